# Optimizing a Trainium2 kernel written in Bass

```python
import math
import jax, jax.numpy as jnp
from jax import lax
import numpy as np

D_MODEL = 1024
BATCH = 16
SEQ = 2048
DEPTH = 4
DEC_BATCH = 8
DEC_SEQ = 64
PAST_LEN = 2048

CHUNK = 64
GM_CHUNK = 128
GM_GROUPS = 4
GM_WIDTH = D_MODEL // 2
GM_HEAD = GM_WIDTH // GM_GROUPS
GLA_HEADS = 4
GLA_KDIM = D_MODEL // 2
GLA_VDIM = D_MODEL
GLA_DK = GLA_KDIM // GLA_HEADS
GLA_DV = GLA_VDIM // GLA_HEADS
GLA_RANK = 16
GLA_TEMP = 16.0
D_FF = 2816
IN_COLS = 2 * GM_WIDTH + 2 * GLA_KDIM + 2 * GLA_VDIM + GLA_RANK + 2 * D_MODEL
ALPHA = (2.0 * DEPTH) ** 0.25
BETA = (8.0 * DEPTH) ** -0.25
EPS = 1e-5

kernel_name = "gmlp_gla_gated_parallel_deepnorm_macaron_stream"


def layer_norm(x, g, b):
    xf = x.astype(jnp.float32)
    mu = jnp.mean(xf, axis=-1, keepdims=True)
    var = jnp.mean(jnp.square(xf - mu), axis=-1, keepdims=True)
    return ((xf - mu) * lax.rsqrt(var + EPS) * g.astype(jnp.float32) + b.astype(jnp.float32)).astype(x.dtype)


def rms_norm(x, g):
    xf = x.astype(jnp.float32)
    return xf * lax.rsqrt(jnp.mean(jnp.square(xf), axis=-1, keepdims=True) + EPS) * g.astype(jnp.float32)


def swiglu(x, w1, w3, w2):
    return (jax.nn.silu(x @ w1) * (x @ w3)) @ w2


def gmlp_spatial(v, ws, bs):
    bn, L, g, dg = v.shape
    c = min(L, GM_CHUNK)
    n = L // c
    mask = jnp.tril(jnp.ones((c, c), dtype=bool))
    w = jnp.where(mask, ws[:, :c, :c], 0.0)
    vc = v.reshape(bn, n, c, g, dg)
    s = jnp.einsum('gts,bnsgd->bntgd', w, vc) + jnp.transpose(bs[:, :c])[None, None, :, :, None]
    return s.reshape(bn, L, g, dg)


def gla_scan(q, k, v, lg, s0, chunk):
    bn, L, h, dk = q.shape
    dv = v.shape[-1]
    n = L // chunk
    mask = jnp.tril(jnp.ones((chunk, chunk), dtype=bool))

    def blocks(a):
        return jnp.moveaxis(a.astype(jnp.float32).reshape(bn, n, chunk, *a.shape[2:]), 1, 0)

    def step(S, xs):
        qc, kc, vc, gc = xs
        b = jnp.cumsum(gc, axis=1)
        qe = qc * jnp.exp(b)
        ke = kc * jnp.exp(-b)
        att = jnp.where(mask, jnp.einsum('bthk,bshk->bhts', qe, ke), 0.0)
        o = jnp.einsum('bhts,bshv->bthv', att, vc) + jnp.einsum('bthk,bhkv->bthv', qe, S)
        bl = b[:, -1]
        S = jnp.exp(bl)[..., None] * S + jnp.einsum('bshk,bshv->bhkv', kc * jnp.exp(bl[:, None] - b), vc)
        return S, o

    S, o = lax.scan(step, s0.astype(jnp.float32), (blocks(q), blocks(k), blocks(v), blocks(lg)))
    return jnp.moveaxis(o, 0, 1).reshape(bn, L, h, dv), S


def mixer(h, w_in, gm_ln_g, gm_ln_b, gm_ws, gm_bs, gla_wa2, gla_ba, gla_norm_g, w_pa, w_pb, w_o, s0):
    bn, L, _ = h.shape
    z = h @ w_in
    sizes = [GM_WIDTH, GM_WIDTH, GLA_KDIM, GLA_KDIM, GLA_VDIM, GLA_VDIM, GLA_RANK, D_MODEL, D_MODEL]
    idx = [int(i) for i in np.cumsum(sizes)[:-1]]
    zu, zv, q, k, vg, gg, glr, ga, gb = jnp.split(z, idx, axis=-1)
    u = jax.nn.gelu(zu, approximate=False)
    vn = layer_norm(jax.nn.gelu(zv, approximate=False), gm_ln_g, gm_ln_b)
    s = gmlp_spatial(vn.reshape(bn, L, GM_GROUPS, GM_HEAD), gm_ws, gm_bs).reshape(bn, L, GM_WIDTH)
    o_a = u * s
    lg = jax.nn.log_sigmoid((glr @ gla_wa2 + gla_ba).astype(jnp.float32)) / GLA_TEMP
    o_b, S = gla_scan(
        q.reshape(bn, L, GLA_HEADS, GLA_DK) * (GLA_DK ** -0.5),
        k.reshape(bn, L, GLA_HEADS, GLA_DK),
        vg.reshape(bn, L, GLA_HEADS, GLA_DV),
        lg.reshape(bn, L, GLA_HEADS, GLA_DK),
        s0, min(L, CHUNK))
    o_b = rms_norm(o_b, gla_norm_g).reshape(bn, L, GLA_VDIM).astype(h.dtype) * jax.nn.silu(gg)
    m = jax.nn.sigmoid(ga) * (o_a @ w_pa) + jax.nn.sigmoid(gb) * (o_b @ w_pb)
    return m @ w_o, S, vn


def setup_inputs(seed: int = 0) -> dict:
    key = jax.random.key(seed)
    ks = jax.random.split(key, 24)
    f32 = jnp.float32

    def nrm(k, shape, scale):
        return jax.random.normal(k, shape, f32) * scale

    return {
        "x_prompt": nrm(ks[0], (BATCH, SEQ, D_MODEL), 1.0),
        "x_sample": nrm(ks[1], (DEC_BATCH, DEC_SEQ, D_MODEL), 1.0),
        "state_gla": nrm(ks[2], (DEPTH, DEC_BATCH, GLA_HEADS, GLA_DK, GLA_DV), 0.3),
        "ln_g": 1.0 + nrm(ks[3], (DEPTH, 3, D_MODEL), 0.02),
        "ln_b": nrm(ks[4], (DEPTH, 3, D_MODEL), 0.02),
        "ffn_w1": nrm(ks[5], (DEPTH, 2, D_MODEL, D_FF), D_MODEL ** -0.5),
        "ffn_w3": nrm(ks[6], (DEPTH, 2, D_MODEL, D_FF), D_MODEL ** -0.5),
        "ffn_w2": nrm(ks[7], (DEPTH, 2, D_FF, D_MODEL), BETA * D_FF ** -0.5),
        "w_in": nrm(ks[8], (DEPTH, D_MODEL, IN_COLS), D_MODEL ** -0.5),
        "gm_ln_g": 1.0 + nrm(ks[9], (DEPTH, GM_WIDTH), 0.02),
        "gm_ln_b": nrm(ks[10], (DEPTH, GM_WIDTH), 0.02),
        "gm_ws": nrm(ks[11], (DEPTH, GM_GROUPS, GM_CHUNK, GM_CHUNK), GM_CHUNK ** -0.5),
        "gm_bs": 1.0 + nrm(ks[12], (DEPTH, GM_GROUPS, GM_CHUNK), 0.02),
        "gla_wa2": nrm(ks[13], (DEPTH, GLA_RANK, GLA_KDIM), GLA_RANK ** -0.5),
        "gla_ba": nrm(ks[14], (DEPTH, GLA_KDIM), 0.1) + 1.0,
        "gla_norm_g": 1.0 + nrm(ks[15], (DEPTH, GLA_DV), 0.02),
        "w_pa": nrm(ks[16], (DEPTH, GM_WIDTH, D_MODEL), GM_WIDTH ** -0.5),
        "w_pb": nrm(ks[17], (DEPTH, GLA_VDIM, D_MODEL), GLA_VDIM ** -0.5),
        "w_o": nrm(ks[18], (DEPTH, D_MODEL, D_MODEL), BETA * D_MODEL ** -0.5),
    }


def layer(x, s0, l, ln_g, ln_b, ffn_w1, ffn_w3, ffn_w2, w_in, gm_ln_g, gm_ln_b, gm_ws, gm_bs,
          gla_wa2, gla_ba, gla_norm_g, w_pa, w_pb, w_o):
    x = layer_norm(ALPHA * x + 0.5 * swiglu(x, ffn_w1[l, 0], ffn_w3[l, 0], ffn_w2[l, 0]), ln_g[l, 0], ln_b[l, 0])
    y, S, vn = mixer(x, w_in[l], gm_ln_g[l], gm_ln_b[l], gm_ws[l], gm_bs[l], gla_wa2[l], gla_ba[l],
                     gla_norm_g[l], w_pa[l], w_pb[l], w_o[l], s0)
    x = layer_norm(ALPHA * x + y, ln_g[l, 1], ln_b[l, 1])
    x = layer_norm(ALPHA * x + 0.5 * swiglu(x, ffn_w1[l, 1], ffn_w3[l, 1], ffn_w2[l, 1]), ln_g[l, 2], ln_b[l, 2])
    return x, S, vn


def reference(x_prompt, x_sample, state_gla, ln_g, ln_b, ffn_w1, ffn_w3, ffn_w2, w_in, gm_ln_g, gm_ln_b,
              gm_ws, gm_bs, gla_wa2, gla_ba, gla_norm_g, w_pa, w_pb, w_o):
    hp = x_prompt
    hs = x_sample
    s_prompt = []
    s_sample = []
    v_sample = []
    s_zero = jnp.zeros((x_prompt.shape[0], GLA_HEADS, GLA_DK, GLA_DV), jnp.float32)
    for l in range(DEPTH):
        hp, Sp, _ = layer(hp, s_zero, l, ln_g, ln_b, ffn_w1, ffn_w3, ffn_w2, w_in, gm_ln_g, gm_ln_b, gm_ws,
                          gm_bs, gla_wa2, gla_ba, gla_norm_g, w_pa, w_pb, w_o)
        hs, Ss, vs = layer(hs, state_gla[l], l, ln_g, ln_b, ffn_w1, ffn_w3, ffn_w2, w_in, gm_ln_g, gm_ln_b,
                           gm_ws, gm_bs, gla_wa2, gla_ba, gla_norm_g, w_pa, w_pb, w_o)
        s_prompt.append(Sp)
        s_sample.append(Ss)
        v_sample.append(vs)
    state_gla_prompt = jnp.stack(s_prompt, axis=0)
    state_gla_sample = jnp.stack(s_sample, axis=0)
    state_gmlp_v_sample = jnp.stack(v_sample, axis=0)
    return (hp, hs, state_gla_prompt, state_gla_sample, state_gmlp_v_sample)
```

```python
import numpy as np
import concourse.bass as bass
import concourse.mybir as mybir
from concourse.bass_utils import run_bass_kernel_spmd

F32 = mybir.dt.float32
BF16 = mybir.dt.bfloat16
AF = mybir.ActivationFunctionType
ALU = mybir.AluOpType

D = 1024
DFF = 2816
NF = DFF // 128
INC = 6160
O_ZU, O_ZV, O_Q, O_K, O_V, O_GG, O_GLR, O_GA, O_GB = 0, 512, 1024, 1536, 2048, 3072, 4096, 4112, 5136
DEPTH = 4
ALPHA = (2.0 * DEPTH) ** 0.25
EPS = 1e-5
TT = 512
N_FILL = 16


def I(name, *a, **kw):
    return (name, a, kw)


def _call(eng, d):
    return getattr(eng, d[0])(*d[1], **d[2])


class Buf:
    __slots__ = ("name", "w", "r")

    def __init__(self, name):
        self.name = name
        self.w = None
        self.r = {}


class Prog:
    ENGS = ("pe", "act", "dve", "pool", "sp")

    def __init__(self, nc):
        self.nc = nc
        self.sem = {}
        self.cnt = {}
        self.waited = {e: {} for e in self.ENGS}
        for e in self.ENGS:
            self.sem[e] = nc.alloc_semaphore(name="s_" + e)
            self.cnt[e] = 0
        self.q = {e: [] for e in self.ENGS}
        self.n_inst = 0

    def dma_sem(self, key):
        sk = ("dma", key)
        if sk not in self.sem:
            self.sem[sk] = self.nc.alloc_semaphore(name="d_" + str(key))
            self.cnt[sk] = 0
        return sk

    def _need(self, e, semkey, val):
        w = self.waited[e]
        if w.get(semkey, 0) >= val:
            return
        sem = self.sem[semkey]
        self.q[e].append(lambda eng, sem=sem, val=val: eng.wait_ge(sem, val))
        w[semkey] = val

    def _deps(self, e, reads, writes, raw_same=True):
        for b in reads:
            if b.w is not None:
                k, v = b.w
                if k == e and not raw_same:
                    continue
                self._need(e, k, v)
        for b in writes:
            if b.w is not None:
                k, v = b.w
                if k != e:
                    self._need(e, k, v)
            for k, v in b.r.items():
                if k != e:
                    self._need(e, k, v)

    def _mark(self, k, v, reads, writes):
        for b in reads:
            if b.r.get(k, 0) < v:
                b.r[k] = v
        for b in writes:
            b.w = (k, v)
            b.r = {}

    def op(self, e, fn, reads=(), writes=()):
        self._deps(e, reads, writes)
        sem = self.sem[e]
        self.q[e].append(lambda eng, fn=fn, sem=sem: _call(eng, fn).then_inc(sem, 1))
        self.cnt[e] += 1
        self._mark(e, self.cnt[e], reads, writes)
        self.n_inst += 1

    def mm(self, fns, reads=(), writes=(), per=None):
        e = "pe"
        self._deps(e, reads, writes, raw_same=False)
        sem = self.sem[e]
        n = len(fns)
        for i, fn in enumerate(fns):
            if per is not None:
                self._deps(e, per[i], (), raw_same=False)
            if i < n - 1:
                self.q[e].append(lambda eng, fn=fn: _call(eng, fn))
            else:
                self.q[e].append(lambda eng, fn=fn, sem=sem: _call(eng, fn).then_inc(sem, 1))
        self.cnt[e] += 1
        allreads = list(reads)
        if per is not None:
            for p_ in per:
                allreads.extend(p_)
        self._mark(e, self.cnt[e], allreads, writes)
        self.n_inst += n

    def dma(self, q, key, out, in_, reads=(), writes=()):
        sk = self.dma_sem(key)
        self._deps(q, reads, writes)
        sem = self.sem[sk]
        self.q[q].append(lambda eng, out=out, in_=in_, sem=sem: eng.dma_start(out=out, in_=in_).then_inc(sem, 16))
        self.cnt[sk] += 16
        self._mark(sk, self.cnt[sk], reads, writes)
        self.n_inst += 1

    def barrier(self, engs=("pe", "act", "dve", "pool")):
        for e in engs:
            for o in engs:
                if o != e and self.cnt[o] > 0:
                    self._need(e, o, self.cnt[o])

    def finish(self):
        for k, v in self.cnt.items():
            if v > 0 and k != "sp":
                self._need("sp", k, v)

    def run(self):
        nc = self.nc
        with nc.Block() as block:
            def mk(e):
                def f(eng):
                    for t in self.q[e]:
                        t(eng)
                return f
            block.tensor(mk("pe"))
            block.scalar(mk("act"))
            block.vector(mk("dve"))
            block.gpsimd(mk("pool"))
            block.sync(mk("sp"))


def build(depth, nseq, seqlen, has_sample, n_w_slots=5):
    nc = bass.Bass("TRN2", target_bir_lowering=False)
    P = Prog(nc)
    NTOK = nseq * seqlen

    def din(name, shape, dt=F32):
        return nc.dram_tensor(name, shape, dt, kind="ExternalInput").ap()

    def dout(name, shape):
        return nc.dram_tensor(name, shape, F32, kind="ExternalOutput").ap()

    xp = din("xp", [max(NTOK, 1), D])
    xs = din("xs", [64, D])
    st0 = din("st0", [depth, 4, 128, 256])
    w_f32 = {
        "w1": din("w1", [depth * 2 * D, DFF]), "w3": din("w3", [depth * 2 * D, DFF]),
        "w2": din("w2", [depth * 2 * DFF, D]), "win": din("win", [depth * D, INC]),
        "wpa": din("wpa", [depth * 512, D]), "wpb": din("wpb", [depth * D, D]), "wo": din("wo", [depth * D, D]),
    }
    lng_d = din("lng", [depth * 24, 128])
    lnb_d = din("lnb", [depth * 24, 128])
    gng_d = din("gng", [depth * 2, 128])
    gmg_d = din("gmg", [depth, 512])
    gmb_d = din("gmb", [depth, 512])
    bs_d = din("bs", [depth, 512])
    ba_d = din("ba", [depth, 512])
    ws_d = din("ws", [depth * 4, 128, 128])
    wa2_d = din("wa2", [depth, 16, 512])

    yp = dout("yp", [max(NTOK, 1), D])
    ys = dout("ys", [64, D])
    sp_out = dout("sp_out", [depth, max(nseq, 1), 4, 128, 256])
    ss_out = dout("ss_out", [depth, 4, 128, 256])
    vs_out = dout("vs_out", [depth, 64, 512])

    w_bf = {k: nc.dram_tensor(k + "_bf", list(v.shape), BF16).ap() for k, v in w_f32.items()}
    w_rows = {"w1": 2 * D, "w3": 2 * D, "w2": 2 * DFF, "win": D, "wpa": 512, "wpb": D, "wo": D}
    scrB = {(l, g): Buf("scr%d_%d" % (l, g)) for l in range(depth) for g in range(3)}
    GRP = {"w1": None, "w3": None, "w2": None, "win": 1, "wpa": 1, "wpb": 1, "wo": 1}

    def emit_conversions():
        for l in range(depth):
            for g in range(3):
                if g == 1:
                    items = [("win", l * D, D), ("wpa", l * 512, 512), ("wpb", l * D, D), ("wo", l * D, D)]
                else:
                    sidx = 0 if g == 0 else 1
                    items = [("w1", (l * 2 + sidx) * D, D), ("w3", (l * 2 + sidx) * D, D), ("w2", (l * 2 + sidx) * DFF, DFF)]
                for (k, r0, nr) in items:
                    step = 256
                    for a in range(0, nr, step):
                        b = min(nr, a + step)
                        P.dma("pool", "cvt%d_%d" % (l, g), w_bf[k][r0 + a:r0 + b, :], w_f32[k][r0 + a:r0 + b, :], writes=[scrB[(l, g)]])

    def sb(name, shape, dt=F32):
        return nc.alloc_sbuf_tensor(name, shape, dt)

    ident = sb("ident", [128, 128]); onesf = sb("onesf", [128, 128])
    ucum = sb("ucum", [128, 128]); lst = sb("lst", [128, 128])
    maskb = sb("maskb", [128, 128], BF16)
    ones_ln = sb("ones_ln", [128, 128], BF16); ones_rms = sb("ones_rms", [128, 128], BF16)
    tmpc = sb("tmpc", [128, 128])
    Bc = Buf("consts")
    P.op("pool", I("memset", onesf[:], 1.0), writes=[Bc])
    P.op("pool", I("memset", ones_ln[:], 1.0 / D), writes=[Bc])
    P.op("pool", I("memset", ones_rms[:], 1.0 / 256), writes=[Bc])
    P.op("pool", I("memset", tmpc[:], -1.0 / 16), writes=[Bc])
    dmy = sb("dmy", [128, 2]); Bdmy = Buf("dmy")
    eps_ln_t = sb("eps_ln_t", [128, 2])
    P.op("pool", I("memset", eps_ln_t[:, 0:1], EPS / (ALPHA * ALPHA)), writes=[Bc])
    P.op("pool", I("memset", eps_ln_t[:, 1:2], EPS), writes=[Bc])
    P.op("pool", I("memset", dmy[:], 1.0), writes=[Bc])

    def preload_ln_table():
        P.op("act", I("activation", out=dmy[:, 1:2], in_=dmy[:, 0:1], func=AF.Ln), reads=[Bc], writes=[Bdmy])
    P.op("pool", I("affine_select", out=ident[:], in_=onesf[:], pattern=[[-1, 128]], compare_op=ALU.is_equal, fill=0.0, base=0, channel_multiplier=1), reads=[Bc], writes=[Bc])
    P.op("pool", I("affine_select", out=ucum[:], in_=tmpc[:], pattern=[[1, 128]], compare_op=ALU.is_ge, fill=0.0, base=0, channel_multiplier=-1), reads=[Bc], writes=[Bc])
    P.op("pool", I("affine_select", out=lst[:], in_=tmpc[:], pattern=[[-1, 128]], compare_op=ALU.is_gt, fill=0.0, base=0, channel_multiplier=1), reads=[Bc], writes=[Bc])
    P.op("pool", I("affine_select", out=maskb[:], in_=onesf[:], pattern=[[1, 128]], compare_op=ALU.is_ge, fill=0.0, base=0, channel_multiplier=-1), reads=[Bc], writes=[Bc])

    banks = [(nc.alloc_psum_tensor("bank%d" % i, [128, 512], F32), Buf("bank%d" % i)) for i in range(8)]
    bank_i = [0]

    def nbank():
        b = banks[bank_i[0] % 6]
        bank_i[0] += 1
        return b

    lngT = sb("lngT", [128, depth * 24]); lnbT = sb("lnbT", [128, depth * 24]); gngT = sb("gngT", [128, depth * 2])
    stage = sb("stage", [128, 128])
    Bst = Buf("stage")
    for (src, dst, n) in ((lng_d, lngT, depth * 24), (lnb_d, lnbT, depth * 24), (gng_d, gngT, depth * 2)):
        P.dma("sp", "misc", stage[0:n, :], src[:, :], writes=[Bst])
        ps, Bp = nbank()
        P.mm([I("transpose", ps[:, 0:n], stage[0:n, :], ident[0:n, 0:n])], reads=[Bst, Bc], writes=[Bp])
        P.op("dve", I("tensor_copy", out=dst[:, 0:n], in_=ps[:, 0:n]), writes=[Bp, Bc])
    WT = sb("WT", [128, depth * 4, 128], BF16)
    wstage = sb("wstage", [128, 128])
    for i in range(depth * 4):
        P.dma("sp", "misc", stage[:, :], ws_d[i], writes=[Bst])
        Bw = Buf("wst")
        P.op("pool", I("affine_select", out=wstage[:], in_=stage[:], pattern=[[-1, 128]], compare_op=ALU.is_ge, fill=0.0, base=0, channel_multiplier=1), reads=[Bst], writes=[Bw])
        ps, Bp = nbank()
        P.mm([I("transpose", ps[:, 0:128], wstage[:, :], ident[:, :])], reads=[Bw, Bc], writes=[Bp])
        P.op("dve", I("tensor_copy", out=WT[:, i, :], in_=ps[:, 0:128]), reads=[], writes=[Bp, Bc, Bst])
    wa2b = sb("wa2b", [16, depth, 512], BF16)
    for l in range(depth):
        P.dma("pool", "misc2", wa2b[:, l, :], wa2_d[l], writes=[Bc])

    S = [sb("S%d" % l, [128, 4, 256]) for l in range(depth)]
    BS = [Buf("S%d" % l) for l in range(depth)]
    Sbf = sb("Sbf", [128, 4, 256], BF16); BSbf = Buf("Sbf")
    xres = sb("xres", [128, 8, TT]); xbf = sb("xbf", [128, 8, TT], BF16)
    Bxr = [Buf("xr%d" % c) for c in range(8)]
    Bxb = [Buf("xb%d" % c) for c in range(8)]
    xio = sb("xio", [128, D]); Bxio = Buf("xio")
    vnf = sb("vnf", [128, 512]); Bvnf = Buf("vnf")
    lc = sb("lc", [128, 4, 512]); Blc = Buf("lc")

    direct = [True]
    blockB = {}
    pool_ok = [False]

    def PL(alt="dve"):
        return "pool" if pool_ok[0] else alt

    slots = [(sb("wslot%d" % i, [128, 4096], BF16), Buf("wslot%d" % i)) for i in range(n_w_slots)]
    slot_i = [0]

    def load_w(key, l, r0, nk, c0, ncols, g=1):
        def last_use(si):
            Bs = slots[si][1]
            if Bs.w is None:
                return -1
            return Bs.r.get("pe", 1 << 60)
        si = min(range(n_w_slots), key=lambda i_: (last_use(i_), (i_ - slot_i[0]) % n_w_slots))
        slot_i[0] = si + 1
        t, B = slots[si]
        v = t[:, 0:nk * ncols].rearrange("p (k n) -> p k n", k=nk)
        src = w_bf[key][r0:r0 + nk * 128, c0:c0 + ncols].rearrange("(k p) n -> p k n", p=128)
        blk = (key, r0, nk, c0, ncols)
        if direct[0]:
            src32 = w_f32[key][r0:r0 + nk * 128, c0:c0 + ncols].rearrange("(k p) n -> p k n", p=128)
            P.dma("pool", "ws%d" % si, v, src32, writes=[B])
            if blk not in blockB:
                blockB[blk] = Buf("blk")
            P.dma("sp", "wb%d" % si, src, v, reads=[B], writes=[blockB[blk]])
        else:
            P.dma("sp", "ws%d" % si, v, src, reads=[blockB[blk]], writes=[B])
        return v, B

    RN = 40700
    R = sb("R", [128, RN], BF16)

    class Carver:
        def __init__(self):
            self.off = 0

        def take(self, n_elems, dt, shape):
            nb = n_elems * (4 if dt == F32 else 2)
            nb = (nb + 31) // 32 * 32
            a = self.off
            self.off += nb // 2
            assert self.off <= RN, ("R overflow", self.off)
            v = R[:, a:a + nb // 2]
            if dt == F32:
                v = v.bitcast(F32)
            v = v[:, 0:n_elems]
            if len(shape) == 3:
                v = v.rearrange("p (a b) -> p a b", a=shape[1])
            return v

    CY_F = 0.5 / ALPHA
    CY_M = 1.0 / ALPHA
    EPS_LN = EPS / (ALPHA * ALPHA)

    ln_sqb = sb("ln_sqb", [128, 8, TT], BF16); Bln_sq = [Buf("lsq%d" % c) for c in range(8)]
    ln_mean = sb("ln_mean", [128, TT]); Bln_mean = Buf("lmean")
    ln_msq = sb("ln_msq", [128, TT]); Bln_msq = Buf("lmsq")
    ln_var = sb("ln_var", [128, TT]); Bln_var = Buf("lvar")
    ln_rstd = sb("ln_rstd", [128, TT]); Bln_rstd = Buf("lrstd")
    pm, Bpm = banks[6]
    pq, Bpq = banks[7]

    def ln_chunk_prep(c, T):
        P.op("act", I("activation", out=ln_sqb[:, c, :T], in_=xres[:, c, :T], func=AF.Square), reads=[Bxr[c]], writes=[Bln_sq[c]])
        P.op(PL(), I("tensor_copy", out=xbf[:, c, :T], in_=xres[:, c, :T]), reads=[Bxr[c]], writes=[Bxb[c]])

    def ln_chunk_stats(c, T):
        P.mm([I("matmul", pm[:, :T], lhsT=ones_ln[:, :], rhs=xbf[:, c, :T], start=(c == 0), stop=(c == 7))], reads=[Bxb[c], Bc], writes=[Bpm])
        P.mm([I("matmul", pq[:, :T], lhsT=ones_ln[:, :], rhs=ln_sqb[:, c, :T], start=(c == 0), stop=(c == 7))], reads=[Bln_sq[c], Bc], writes=[Bpq])

    def layer_norm(l, s, T):
        kw_ps, kw_B = nbank()

        def keep_warm(after):
            P.mm([I("matmul", kw_ps[:, 0:16], lhsT=ones_ln[:, :], rhs=ones_ln[:, 0:16], start=True, stop=True)], reads=[after, Bc], writes=[kw_B])

        P.mm([I("matmul", kw_ps[:, :T], lhsT=ones_ln[:, :], rhs=ln_sqb[:, 7, :T], start=True, stop=True) for _ in range(N_FILL)],
             reads=[Bln_sq[7], Bc], writes=[kw_B])
        P.op("act", I("activation", out=ln_msq[:, :T], in_=pm[:, :T], func=AF.Square), writes=[Bln_msq, Bpm])
        P.op("act", I("activation", out=ln_mean[:, :T], in_=pm[:, :T], func=AF.Copy), writes=[Bln_mean, Bpm])
        keep_warm(Bln_mean)
        P.op("dve", I("tensor_tensor", out=ln_var[:, :T], in0=pq[:, :T], in1=ln_msq[:, :T], op=ALU.subtract), reads=[Bln_msq], writes=[Bln_var, Bpq])
        P.op("act", I("activation", out=ln_var[:, :T], in_=ln_var[:, :T], func=AF.Ln, bias=eps_ln_t[:, 0:1], scale=1.0), reads=[Bln_var, Bc], writes=[Bln_var])
        keep_warm(Bln_var)
        P.op("act", I("activation", out=ln_rstd[:, :T], in_=ln_var[:, :T], func=AF.Exp, scale=-0.5), reads=[Bln_var], writes=[Bln_rstd])
        keep_warm(Bln_rstd)
        col = (l * 3 + s) * 8
        gcs = [lngT[:, col + c:col + c + 1] for c in range(8)]
        bcs = [lnbT[:, col + c:col + c + 1] for c in range(8)]

        def sub(c):
            P.op("dve", I("tensor_tensor", out=xres[:, c, :T], in0=xres[:, c, :T], in1=ln_mean[:, :T], op=ALU.subtract), reads=[Bxr[c], Bln_mean], writes=[Bxr[c]])

        def mul(c):
            P.op("dve", I("tensor_tensor", out=xres[:, c, :T], in0=xres[:, c, :T], in1=ln_rstd[:, :T], op=ALU.mult), reads=[Bxr[c], Bln_rstd], writes=[Bxr[c]])
            P.op("act", I("activation", out=xbf[:, c, :T], in_=xres[:, c, :T], func=AF.Identity, scale=gcs[c], bias=bcs[c]), reads=[Bxr[c], Bc], writes=[Bxb[c]])

        for c in range(4):
            sub(c)
        for c in range(4):
            mul(c)
            sub(c + 4)
        for c in range(4, 8):
            mul(c)
        for c in range(8):
            P.op("dve", I("tensor_scalar", out=xres[:, c, :T], in0=xres[:, c, :T], scalar1=gcs[c], scalar2=bcs[c], op0=ALU.mult, op1=ALU.add), reads=[Bxr[c], Bc], writes=[Bxr[c]])

    Bh = [Buf("h%d" % j) for j in range(NF)]
    Bsil = [Buf("sil0"), Buf("sil1"), Buf("sil2")]

    def ffn(l, s, T):
        cv = Carver()
        hT = cv.take(NF * TT, BF16, [128, NF, TT])
        sil = [cv.take(TT, F32, [128, TT]) for _ in range(3)]
        r0 = (l * 2 + s) * D
        for blk in range(0, NF, 4):
            nch = min(4, NF - blk)
            w1v, B1 = load_w("w1", l, r0, 8, blk * 128, nch * 128, g=2 * s)
            w3v, B3 = load_w("w3", l, r0, 8, blk * 128, nch * 128, g=2 * s)
            for jj in range(nch):
                j = blk + jj
                pa, Bpa = nbank()
                P.mm([I("matmul", pa[:, :T], lhsT=w1v[:, k, jj * 128:(jj + 1) * 128], rhs=xbf[:, k, :T], start=(k == 0), stop=(k == 7)) for k in range(8)], reads=[B1], per=[[Bxb[k]] for k in range(8)], writes=[Bpa])
                pb, Bpb = nbank()
                P.mm([I("matmul", pb[:, :T], lhsT=w3v[:, k, jj * 128:(jj + 1) * 128], rhs=xbf[:, k, :T], start=(k == 0), stop=(k == 7)) for k in range(8)], reads=[B3], per=[[Bxb[k]] for k in range(8)], writes=[Bpb])
                st, Bs_ = sil[j % 3], Bsil[j % 3]
                P.op("act", I("activation", out=st[:, :T], in_=pa[:, :T], func=AF.Silu), writes=[Bs_, Bpa])
                P.op("dve", I("tensor_tensor", out=hT[:, j, :T], in0=pb[:, :T], in1=st[:, :T], op=ALU.mult), reads=[Bs_], writes=[Bh[j], Bpb])
        r2 = (l * 2 + s) * DFF
        preload_ln_table()
        for c in range(8):
            halves = []
            for (k0, nk) in ((0, 16), (16, 6)):
                halves.append(load_w("w2", l, r2 + k0 * 128, nk, c * 128, 128, g=2 * s))
            py, Bpy = nbank()
            fns = []
            per = []
            for hi, (k0, nk) in enumerate(((0, 16), (16, 6))):
                wv = halves[hi][0]
                for k in range(nk):
                    fns.append(I("matmul", py[:, :T], lhsT=wv[:, k, :], rhs=hT[:, k0 + k, :T], start=(k0 + k == 0), stop=(k0 + k == NF - 1)))
                    per.append([Bh[k0 + k], halves[hi][1]])
            P.mm(fns, reads=[], writes=[Bpy], per=per)
            P.op("dve", I("scalar_tensor_tensor", out=xres[:, c, :T], in0=py[:, :T], scalar=CY_F, in1=xres[:, c, :T], op0=ALU.mult, op1=ALU.add), reads=[Bxr[c]], writes=[Bxr[c], Bpy])
            ln_chunk_prep(c, T)
            if c > 0:
                ln_chunk_stats(c - 1, T)
        ln_chunk_stats(7, T)

    def mixer(l, T, first, s0_ap, s_out_ap, v_out_ap):
        C = min(T, 128)
        NCH = T // C
        cv = Carver()
        uT = cv.take(4 * TT, BF16, [128, 4, TT]); Bu = [Buf("u%d" % g) for g in range(4)]
        gv = [cv.take(512, F32, [128, 512]) for _ in range(2)] + [xio[:, 0:512], xio[:, 512:1024]]
        Bgv = [Buf("gv0"), Buf("gv1"), Bxio, Bxio]
        vnb = cv.take(4 * 512, BF16, [128, 4, 512]); Bvnb = [Buf("vnb%d" % i) for i in range(4)]
        stt = cv.take(32, F32, [128, 4, 8]); Bstt = [Buf("stt%d" % i) for i in range(4)]
        mv = cv.take(16, F32, [128, 4, 4]); Bmv = Buf("mv")
        qk = cv.take(8 * TT, BF16, [128, 8, TT]); Bq = Buf("qT"); Bk = Buf("kT")
        qT = qk[:, 0:4, :]; kT = qk[:, 4:8, :]; mT = qk
        sgg = cv.take(8 * TT, BF16, [128, 8, TT]); Bsgg = [Buf("sgg%d" % i) for i in range(4)]
        glrT = cv.take(TT, BF16, [128, TT]); Bglr = Buf("glr")
        vtok = cv.take(4 * 1024, BF16, [128, 4, 1024]); Bvt = [Buf("vt%d" % i) for i in range(4)]
        kd = cv.take(4 * 512, BF16, [128, 4, 512]); Bkd = [Buf("kd%d" % i) for i in range(4)]
        lgp = [cv.take(512, F32, [128, 512]) for _ in range(2)]; Blg = [Buf("lgp0"), Buf("lgp1")]
        ekd = [cv.take(512, F32, [128, 512]) for _ in range(2)]; Bekd = [Buf("ekd0"), Buf("ekd1")]
        eb = [cv.take(512, F32, [128, 4, 128]) for _ in range(2)]; Beb = [Buf("eb0"), Buf("eb1")]
        enb = [cv.take(512, F32, [128, 4, 128]) for _ in range(2)]; Benb = [Buf("enb0"), Buf("enb1")]
        ebl = [cv.take(16, F32, [128, 4, 4]) for _ in range(2)]; Bebl = [Buf("ebl0"), Buf("ebl1")]
        qe = [cv.take(512, BF16, [128, 4, 128]) for _ in range(2)]; Bqe = [Buf("qe0"), Buf("qe1")]
        ke = [cv.take(512, BF16, [128, 4, 128]) for _ in range(2)]; Bke = [Buf("ke0"), Buf("ke1")]
        attm = [cv.take(512, BF16, [128, 4, 128]) for _ in range(2)]; Batt = [Buf("att0"), Buf("att1")]
        sq = [cv.take(1024, BF16, [128, 8, 128]) for _ in range(2)]; Bsq = [Buf("osq0"), Buf("osq1")]
        rsd = [cv.take(512, F32, [128, 4, 128]) for _ in range(2)]; Brsd = [Buf("rsd0"), Buf("rsd1")]
        spt = rsd[0]; Bspt = Brsd[0]
        otmp = [cv.take(1024, F32, [128, 8, 128]) for _ in range(2)]; Bot = [Buf("ot0"), Buf("ot1")]
        sga = vnf; Bsga = Bvnf
        m1s = gv; Bm1s = Bgv
        Bm = [Buf("m%d" % c) for c in range(8)]
        assert cv.off <= RN, cv.off
        r0 = l * D

        for i, src in enumerate((gmg_d, gmb_d, bs_d, ba_d)):
            P.dma("sp", "lc", lc[:, i, :], src[l:l + 1, :].partition_broadcast(128), writes=[Blc])

        def proj_fm(c0, nchunks, evac):
            for b0 in range(0, nchunks, 4):
                nb = min(4, nchunks - b0)
                wv, Bw = load_w("win", l, r0, 8, c0 + b0 * 128, nb * 128)
                for jj in range(nb):
                    ps, Bp = nbank()
                    P.mm([I("matmul", ps[:, :T], lhsT=wv[:, k, jj * 128:(jj + 1) * 128], rhs=xbf[:, k, :T], start=(k == 0), stop=(k == 7)) for k in range(8)], reads=[Bw], per=[[Bxb[k]] for k in range(8)], writes=[Bp])
                    evac(b0 + jj, ps, Bp)

        proj_fm(O_ZU, 4, lambda j, ps, Bp: P.op("act", I("activation", out=uT[:, j, :T], in_=ps[:, :T], func=AF.Gelu), writes=[Bu[j], Bp]))
        wzv, Bwzv = load_w("win", l, r0, 8, O_ZV, 512)
        def zv1(i):
            ps, Bp = nbank()
            P.mm([I("matmul", ps[:C, :], lhsT=xbf[:, k, i * C:(i + 1) * C], rhs=wzv[:, k, :], start=(k == 0), stop=(k == 7)) for k in range(8)], reads=[Bwzv], writes=[Bp], per=[[Bxb[k]] for k in range(8)])
            g_ = gv[i]; Bg_ = Bgv[i]; st_ = stt[:, i, :]; mv_ = mv[:, i, :]
            P.op("act", I("activation", out=g_[:C, :], in_=ps[:C, :], func=AF.Gelu), writes=[Bg_, Bp])
            P.op("dve", I("bn_stats", out=st_[:C, 0:6], in_=g_[:C, :]), reads=[Bg_], writes=[Bstt[i]])
            P.op("dve", I("bn_aggr", out=mv_[:C, 0:2], in_=st_[:C, 0:6]), reads=[Bstt[i]], writes=[Bmv])
            P.op("dve", I("tensor_scalar", out=mv_[:C, 2:3], in0=mv_[:C, 1:2], scalar1=EPS, scalar2=None, op0=ALU.add), reads=[Bmv], writes=[Bmv])

        def zv2(i):
            g_ = gv[i]; Bg_ = Bgv[i]; mv_ = mv[:, i, :]
            P.op("dve", I("tensor_scalar", out=g_[:C, :], in0=g_[:C, :], scalar1=mv_[:C, 0:1], scalar2=mv_[:C, 3:4], op0=ALU.subtract, op1=ALU.mult), reads=[Bg_, Bmv], writes=[Bg_])
            P.op("dve", I("tensor_tensor", out=g_[:C, :], in0=g_[:C, :], in1=lc[:C, 0, :], op=ALU.mult), reads=[Bg_, Blc], writes=[Bg_])
            if v_out_ap is not None:
                P.op("dve", I("tensor_tensor", out=vnf[:C, :], in0=g_[:C, :], in1=lc[:C, 1, :], op=ALU.add), reads=[Bg_, Blc], writes=[Bvnf])
                P.dma("sp", "vout", v_out_ap, vnf[:C, :], reads=[Bvnf])
            P.op("dve", I("tensor_tensor", out=vnb[:C, i, :], in0=g_[:C, :], in1=lc[:C, 1, :], op=ALU.add), reads=[Bg_, Blc], writes=[Bvnb[i]])

        for i in range(NCH):
            zv1(i)
        proj_fm(O_Q, 4, lambda j, ps, Bp: P.op("act", I("activation", out=qT[:, j, :T], in_=ps[:, :T], func=AF.Copy, scale=128.0 ** -0.5), writes=[Bq, Bp]))
        P.op("act", I("activation", out=mv[:C, 0:NCH, 2:3], in_=mv[:C, 0:NCH, 2:3], func=AF.Sqrt), reads=[Bmv], writes=[Bmv])
        P.op("dve", I("reciprocal", out=mv[:C, 0:NCH, 3:4], in_=mv[:C, 0:NCH, 2:3]), reads=[Bmv], writes=[Bmv])
        proj_fm(O_K, 4, lambda j, ps, Bp: P.op("act", I("activation", out=kT[:, j, :T], in_=ps[:, :T], func=AF.Copy), writes=[Bk, Bp]))
        for i in range(NCH):
            zv2(i)
        wgl, Bwgl = load_w("win", l, r0, 8, O_GLR, 16)
        ps, Bp = nbank()
        P.mm([I("matmul", ps[:16, :T], lhsT=wgl[:, k, :], rhs=xbf[:, k, :T], start=(k == 0), stop=(k == 7)) for k in range(8)], reads=[Bwgl], per=[[Bxb[k]] for k in range(8)], writes=[Bp])
        P.op("act", I("activation", out=glrT[:16, :T], in_=ps[:16, :T], func=AF.Copy), writes=[Bglr, Bp])

        fillers = []
        gg_state = {}

        def gg_task(j):
            def run():
                blk = j // 4
                if blk not in gg_state:
                    gg_state[blk] = load_w("win", l, r0, 8, O_GG + blk * 512, 512)
                wv, Bw = gg_state[blk]
                jj = j % 4
                ps, Bp = nbank()
                P.mm([I("matmul", ps[:, :T], lhsT=wv[:, k, jj * 128:(jj + 1) * 128], rhs=xbf[:, k, :T], start=(k == 0), stop=(k == 7)) for k in range(8)], reads=[Bw], per=[[Bxb[k]] for k in range(8)], writes=[Bp])
                P.op("act", I("activation", out=sgg[:, j, :T], in_=ps[:, :T], func=AF.Silu), writes=Bsgg + [Bp])
            return run

        def spatial_task(i):
            def run():
                ps, Bp = nbank()
                for g in range(4):
                    P.mm([I("matmul", ps[:, g * C:(g + 1) * C], lhsT=vnb[:C, i, g * 128:(g + 1) * 128], rhs=WT[:C, l * 4 + g, :C], start=True, stop=True)], reads=[Bvnb[i], Bc], writes=[Bp])
                pv = ps[:, 0:4 * C].rearrange("p (g c) -> p g c", g=4)
                P.op("dve", I("tensor_tensor", out=spt[:, :, :C], in0=pv, in1=lc[:, 2, :].rearrange("p (g c) -> p g c", g=4)[:, :, :C], op=ALU.add), reads=[Blc], writes=[Bspt, Bp])
                P.op("dve", I("tensor_tensor", out=uT[:, :, i * C:(i + 1) * C], in0=uT[:, :, i * C:(i + 1) * C], in1=spt[:, :, :C], op=ALU.mult), reads=[Bspt] + Bu, writes=Bu)
            return run

        mg_state = {}

        def merge_p1(hb, cc):
            def run():
                if hb not in mg_state:
                    mg_state[hb] = (load_w("wpa", l, l * 512, 4, hb * 512, 512), load_w("win", l, r0, 8, O_GA + hb * 512, 512))
                (wpa_v, Bwpa), (wga_v, Bwga) = mg_state[hb]
                cs = slice(cc * 128, (cc + 1) * 128)
                pg, Bpg = nbank()
                P.mm([I("matmul", pg[:, :T], lhsT=wga_v[:, k, cs], rhs=xbf[:, k, :T], start=(k == 0), stop=(k == 7)) for k in range(8)], reads=[Bwga], per=[[Bxb[k]] for k in range(8)], writes=[Bpg])
                pp, Bpp = nbank()
                P.mm([I("matmul", pp[:, :T], lhsT=wpa_v[:, k, cs], rhs=uT[:, k, :T], start=(k == 0), stop=(k == 3)) for k in range(4)], reads=[Bwpa], writes=[Bpp], per=[[Bu[k]] for k in range(4)])
                P.op("act", I("activation", out=sga[:, :T], in_=pg[:, :T], func=AF.Sigmoid), writes=[Bsga, Bpg])
                P.op("dve", I("tensor_tensor", out=m1s[cc][:, :T], in0=pp[:, :T], in1=sga[:, :T], op=ALU.mult), reads=[Bsga], writes=[Bm1s[cc], Bpp])
            return run

        fill_plan = {}
        fill_plan.setdefault(0, []).extend(gg_task(j) for j in range(4))
        fill_plan.setdefault(1, []).extend(gg_task(j) for j in range(4, 8))
        fill_plan.setdefault(2, []).extend(spatial_task(i) for i in range(NCH))
        fill_plan.setdefault(NCH + 1, []).extend(merge_p1(0, cc) for cc in range(2))
        fill_plan.setdefault(NCH + 2, []).extend(merge_p1(0, cc) for cc in range(2, 4))
        pending_tasks = []
        if first:
            if s0_ap is None:
                P.op(PL(), I("memset", S[l][:], 0.0), writes=[BS[l]])
            else:
                P.dma("sp", "s0", S[l][:], s0_ap.rearrange("h k v -> k h v"), writes=[BS[l]])
        P.op("act", I("activation", out=Sbf[:], in_=S[l][:], func=AF.Copy), reads=[BS[l]], writes=[BSbf])
        wk, Bwk = load_w("win", l, r0, 8, O_K, 512)
        wv0, Bwv0 = load_w("win", l, r0, 8, O_V, 512)
        wv1, Bwv1 = load_w("win", l, r0, 8, O_V + 512, 512)
        held = {}

        def stA(i):
            b2 = i % 2
            tok = slice(i * C, (i + 1) * C)
            for hv, (wv_, Bwv_) in enumerate(((wv0, Bwv0), (wv1, Bwv1))):
                ps, Bp = nbank()
                P.mm([I("matmul", ps[:C, :], lhsT=xbf[:, k, tok], rhs=wv_[:, k, :], start=(k == 0), stop=(k == 7)) for k in range(8)], reads=[Bwv_], per=[[Bxb[k]] for k in range(8)], writes=[Bp])
                if hv == 0:
                    P.op("act", I("activation", out=vtok[:C, i, 0:512], in_=ps[:C, :], func=AF.Copy), writes=[Bvt[i], Bp])
                else:
                    P.op("dve", I("tensor_copy", out=vtok[:C, i, 512:1024], in_=ps[:C, :]), writes=[Bvt[i], Bp])
            ps, Bp = nbank()
            P.mm([I("matmul", ps[:C, :], lhsT=glrT[:16, tok], rhs=wa2b[:, l, :], start=True, stop=True)], reads=[Bglr, Bc], writes=[Bp])
            P.op("dve", I("tensor_tensor", out=lgp[b2][:C, :], in0=ps[:C, :], in1=lc[:C, 3, :], op=ALU.add), reads=[Blc], writes=[Blg[b2], Bp])
            P.op("act", I("activation", out=lgp[b2][:C, :], in_=lgp[b2][:C, :], func=AF.Exp, scale=-1.0), reads=[Blg[b2]], writes=[Blg[b2]])
            P.op("act", I("activation", out=lgp[b2][:C, :], in_=lgp[b2][:C, :], func=AF.Ln, bias=1.0, scale=1.0), reads=[Blg[b2]], writes=[Blg[b2]])

        def stB(i):
            b2 = i % 2
            tok = slice(i * C, (i + 1) * C)
            ps, Bp = nbank()
            P.mm([I("matmul", ps[:C, :], lhsT=lst[:C, :C], rhs=lgp[b2][:C, :], start=True, stop=True)], reads=[Blg[b2], Bc], writes=[Bp])
            P.op("act", I("activation", out=ekd[b2][:C, :], in_=ps[:C, :], func=AF.Exp), writes=[Bekd[b2], Bp])
            ps2, Bp2 = nbank()
            for h in range(4):
                P.mm([I("matmul", ps2[:, h * C:(h + 1) * C], lhsT=lgp[b2][:C, h * 128:(h + 1) * 128], rhs=ucum[:C, :C], start=True, stop=True)], reads=[Blg[b2], Bc], writes=[Bp2])
            pv = ps2[:, 0:4 * C].rearrange("p (h c) -> p h c", h=4)
            P.op("act", I("activation", out=eb[b2][:, :, :C], in_=pv, func=AF.Exp), writes=[Beb[b2], Bp2])
            P.op("act", I("activation", out=enb[b2][:, :, :C], in_=pv, func=AF.Exp, scale=-1.0), writes=[Benb[b2], Bp2])
            P.op("act", I("activation", out=ebl[b2][:, :, 0:1], in_=pv[:, :, C - 1:C], func=AF.Exp), writes=[Bebl[b2], Bp2])
            ps3, Bp3 = nbank()
            P.mm([I("matmul", ps3[:C, :], lhsT=xbf[:, k, tok], rhs=wk[:, k, :], start=(k == 0), stop=(k == 7)) for k in range(8)], reads=[Bwk], per=[[Bxb[k]] for k in range(8)], writes=[Bp3])
            P.op("dve", I("tensor_tensor", out=kd[:C, i, :], in0=ps3[:C, :], in1=ekd[b2][:C, :], op=ALU.mult), reads=[Bekd[b2]], writes=[Bkd[i], Bp3])
            P.op("dve", I("tensor_tensor", out=qe[b2][:, :, :C], in0=qT[:, :, tok], in1=eb[b2][:, :, :C], op=ALU.mult), reads=[Bq, Beb[b2]], writes=[Bqe[b2]])
            P.op(PL(), I("tensor_tensor", out=ke[b2][:, :, :C], in0=kT[:, :, tok], in1=enb[b2][:, :, :C], op=ALU.mult), reads=[Bk, Benb[b2]], writes=[Bke[b2]])

        def stC(i):
            b2 = i % 2
            ps, Bp = nbank()
            for h in range(4):
                P.mm([I("matmul", ps[:C, h * C:(h + 1) * C], lhsT=ke[b2][:, h, :C], rhs=qe[b2][:, h, :C], start=True, stop=True)], reads=[Bke[b2], Bqe[b2]], writes=[Bp])
            pv = ps[:C, 0:4 * C].rearrange("p (h c) -> p h c", h=4)
            P.op("dve", I("tensor_tensor", out=attm[b2][:C, :, :C], in0=pv, in1=maskb[:C, :C].unsqueeze(1).to_broadcast([C, 4, C]), op=ALU.mult), reads=[Bc], writes=[Batt[b2], Bp])

        def stD(i):
            b2 = i % 2
            po = [nbank(), nbank()]
            for j in range(8):
                h = j // 2
                pj, Bpj = po[j // 4]
                jc = (j % 4) * C
                P.mm([I("matmul", pj[:, jc:jc + C], lhsT=vtok[:C, i, j * 128:(j + 1) * 128], rhs=attm[b2][:C, h, :C], start=True, stop=False),
                      I("matmul", pj[:, jc:jc + C], lhsT=Sbf[:, h, (j % 2) * 128:(j % 2 + 1) * 128], rhs=qe[b2][:, h, :C], start=False, stop=True)],
                     reads=[Bvt[i], Batt[b2], BSbf, Bqe[b2]], writes=[Bpj])
            pS = [nbank(), nbank()]
            for h in range(4):
                pj, Bpj = pS[h // 2]
                P.mm([I("matmul", pj[:, (h % 2) * 256:(h % 2 + 1) * 256], lhsT=kd[:C, i, h * 128:(h + 1) * 128], rhs=vtok[:C, i, h * 256:(h + 1) * 256], start=True, stop=True)], reads=[Bkd[i], Bvt[i]], writes=[Bpj])
            for h in range(4):
                pj, Bpj = pS[h // 2]
                P.op("dve", I("scalar_tensor_tensor", out=S[l][:, h, :], in0=S[l][:, h, :], scalar=ebl[b2][:, h, 0:1], in1=pj[:, (h % 2) * 256:(h % 2 + 1) * 256], op0=ALU.mult, op1=ALU.add), reads=[Bebl[b2], BS[l]], writes=[BS[l], Bpj])
            P.op("act", I("activation", out=Sbf[:], in_=S[l][:], func=AF.Copy), reads=[BS[l]], writes=[BSbf])
            for hh in range(2):
                pj, Bpj = po[hh]
                pv = pj[:, 0:4 * C].rearrange("p (j c) -> p j c", j=4)
                P.op("act", I("activation", out=sq[b2][:, hh * 4:(hh + 1) * 4, :C], in_=pv, func=AF.Square), writes=[Bsq[b2], Bpj])
                P.op("act", I("activation", out=otmp[b2][:, hh * 4:(hh + 1) * 4, :C], in_=pv, func=AF.Copy), writes=[Bot[b2], Bpj])

        def stE(i):
            b2 = i % 2
            tok = slice(i * C, (i + 1) * C)
            pr, Bpr = nbank()
            for h in range(4):
                P.mm([I("matmul", pr[:, h * C:(h + 1) * C], lhsT=ones_rms[:, :], rhs=sq[b2][:, 2 * h, :C], start=True, stop=False),
                      I("matmul", pr[:, h * C:(h + 1) * C], lhsT=ones_rms[:, :], rhs=sq[b2][:, 2 * h + 1, :C], start=False, stop=True)], reads=[Bsq[b2], Bc], writes=[Bpr])
            pv = pr[:, 0:4 * C].rearrange("p (h c) -> p h c", h=4)
            P.op("act", I("activation", out=rsd[b2][:, :, :C], in_=pv, func=AF.Ln, bias=eps_ln_t[:, 1:2], scale=1.0), reads=[Bc], writes=[Brsd[b2], Bpr])
            P.op("act", I("activation", out=rsd[b2][:, :, :C], in_=rsd[b2][:, :, :C], func=AF.Exp, scale=-0.5), reads=[Brsd[b2]], writes=[Brsd[b2]])
            for par in range(2):
                ov = otmp[b2][:, :, :C].rearrange("p (h two) c -> p h two c", two=2)[:, :, par, :]
                P.op("dve", I("scalar_tensor_tensor", out=ov, in0=ov, scalar=gngT[:, l * 2 + par:l * 2 + par + 1], in1=rsd[b2][:, :, :C], op0=ALU.mult, op1=ALU.mult), reads=[Brsd[b2], Bot[b2], Bc], writes=[Bot[b2]])
            P.op(PL(), I("tensor_tensor", out=sgg[:, :, tok], in0=sgg[:, :, tok], in1=otmp[b2][:, :, :C], op=ALU.mult), reads=[Bot[b2], Bsgg[i]], writes=[Bsgg[i]])

        stages = [stA, stB, stC, stD, stE]
        for step in range(NCH + len(stages) - 1):
            for si in reversed(range(len(stages))):
                i = step - si
                if 0 <= i < NCH:
                    stages[si](i)
            for task in fill_plan.pop(step, ()):
                task()
        for k_ in sorted(fill_plan):
            pending_tasks.extend(fill_plan[k_])
        if s_out_ap is not None:
            P.dma("sp", "sout%d" % l, s_out_ap.rearrange("h k v -> k h v"), S[l][:], reads=[BS[l]])
        for t_ in pending_tasks:
            t_()
        del pending_tasks[:]
        for hb in range(2):
            if hb == 1:
                for cc in range(4):
                    merge_p1(1, cc)()
            wpb_v, Bwpb = load_w("wpb", l, l * D, 8, hb * 512, 512)
            wgb_v, Bwgb = load_w("win", l, r0, 8, O_GB + hb * 512, 512)
            for cc in range(4):
                c = hb * 4 + cc
                cs = slice(cc * 128, (cc + 1) * 128)
                pg2, Bpg2 = nbank()
                P.mm([I("matmul", pg2[:, :T], lhsT=wgb_v[:, k, cs], rhs=xbf[:, k, :T], start=(k == 0), stop=(k == 7)) for k in range(8)], reads=[Bwgb], per=[[Bxb[k]] for k in range(8)], writes=[Bpg2])
                pp2, Bpp2 = nbank()
                P.mm([I("matmul", pp2[:, :T], lhsT=wpb_v[:, k, cs], rhs=sgg[:, k, :T], start=(k == 0), stop=(k == 7)) for k in range(8)], reads=Bsgg + [Bwpb], writes=[Bpp2])
                P.op("act", I("activation", out=sga[:, :T], in_=pg2[:, :T], func=AF.Sigmoid), writes=[Bsga, Bpg2])
                P.op("dve", I("tensor_tensor", out=sga[:, :T], in0=pp2[:, :T], in1=sga[:, :T], op=ALU.mult), reads=[Bsga], writes=[Bsga, Bpp2])
                P.op("dve", I("tensor_tensor", out=mT[:, c, :T], in0=m1s[cc][:, :T], in1=sga[:, :T], op=ALU.add), reads=[Bm1s[cc], Bsga], writes=[Bm[c], Bq, Bk])
        preload_ln_table()
        for hb in range(2):
            wo_v, Bwo = load_w("wo", l, l * D, 8, hb * 512, 512)
            for cc in range(4):
                c = hb * 4 + cc
                cs = slice(cc * 128, (cc + 1) * 128)
                py, Bpy = nbank()
                P.mm([I("matmul", py[:, :T], lhsT=wo_v[:, k, cs], rhs=mT[:, k, :T], start=(k == 0), stop=(k == 7)) for k in range(8)], reads=[Bwo], writes=[Bpy], per=[[Bm[k]] for k in range(8)])
                P.op("dve", I("scalar_tensor_tensor", out=xres[:, c, :T], in0=py[:, :T], scalar=CY_M, in1=xres[:, c, :T], op0=ALU.mult, op1=ALU.add), reads=[Bxr[c]], writes=[Bxr[c], Bpy])
                ln_chunk_prep(c, T)
                if c > 0:
                    ln_chunk_stats(c - 1, T)
        ln_chunk_stats(7, T)

    XST0 = 16000
    xst = R[:, XST0:XST0 + 8192].bitcast(F32).rearrange("p (a b) -> p a b", a=4)
    Bxst = Buf("xst")

    def run_tile(T, x_src, y_dst, first, s0, s_out, v_out, prefetched=False, nxt=None):
        C = min(T, 128)
        for i in range(T // C):
            if prefetched:
                src_t, Bsrc = xst[:, i, :], Bxst
            else:
                P.dma("sp", "xin", xio[:C, :], x_src[i * C:(i + 1) * C, :], writes=[Bxio])
                src_t, Bsrc = xio, Bxio
            for half in range(2):
                ps, Bp = nbank()
                for cc in range(4):
                    c = half * 4 + cc
                    P.mm([I("transpose", ps[:, cc * C:(cc + 1) * C], src_t[:C, c * 128:(c + 1) * 128], ident[:C, :C])], reads=[Bsrc, Bc], writes=[Bp])
                pv = ps[:, 0:4 * C].rearrange("p (a b) -> p a b", a=4)
                P.op("act", I("activation", out=xres[:, half * 4:(half + 1) * 4, i * C:(i + 1) * C], in_=pv, func=AF.Copy), writes=Bxr[half * 4:(half + 1) * 4] + [Bp])
                P.op("dve", I("tensor_copy", out=xbf[:, half * 4:(half + 1) * 4, i * C:(i + 1) * C], in_=pv), writes=Bxb[half * 4:(half + 1) * 4] + [Bp])
        for l in range(depth):
            ffn(l, 0, T)
            layer_norm(l, 0, T)
            mixer(l, T, first, None if s0 is None else s0[l], None if s_out is None else s_out[l], None if v_out is None else v_out[l])
            layer_norm(l, 1, T)
            if l == depth - 1 and nxt is not None:
                nx_src, nT = nxt
                nC = min(nT, 128)
                for i in range(nT // nC):
                    P.dma("sp", "xpre", xst[:nC, i, :], nx_src[i * nC:(i + 1) * nC, :], reads=Bxb, writes=[Bxst])
            ffn(l, 1, T)
            layer_norm(l, 2, T)
        for i in range(T // C):
            for half in range(2):
                ps, Bp = nbank()
                for cc in range(4):
                    c = half * 4 + cc
                    P.mm([I("transpose", ps[:C, cc * 128:(cc + 1) * 128], xres[:, c, i * C:(i + 1) * C], ident[:, :])], reads=[Bxr[c], Bc], writes=[Bp])
                P.op("act" if half == 0 else "dve", (I("activation", out=xio[:C, half * 512:(half + 1) * 512], in_=ps[:C, :], func=AF.Copy)) if half == 0 else (I("tensor_copy", out=xio[:C, half * 512:(half + 1) * 512], in_=ps[:C, :])), writes=[Bxio, Bp])
            P.dma("sp", "yout", y_dst[i * C:(i + 1) * C, :], xio[:C, :], reads=[Bxio])

    ntile = seqlen // TT
    tiles = []
    for sq_i in range(nseq):
        for t in range(ntile):
            r = sq_i * seqlen + t * TT
            last = (t == ntile - 1)
            tiles.append(dict(T=TT, x=xp[r:r + TT, :], y=yp[r:r + TT, :], first=(t == 0), s0=None,
                              s_out=[sp_out[l, sq_i] for l in range(depth)] if last else None, v_out=None))
    if has_sample:
        tiles.append(dict(T=64, x=xs, y=ys, first=True, s0=[st0[l] for l in range(depth)],
                          s_out=[ss_out[l] for l in range(depth)], v_out=[vs_out[l] for l in range(depth)]))
    for ti, tl in enumerate(tiles):
        direct[0] = (ti == 0)
        nxt = (tiles[ti + 1]["x"], tiles[ti + 1]["T"]) if ti + 1 < len(tiles) else None
        run_tile(tl["T"], tl["x"], tl["y"], tl["first"], tl["s0"], tl["s_out"], tl["v_out"], prefetched=(ti > 0), nxt=nxt)
    P.finish()
    P.run()
    print("sbuf bytes remaining", nc.sbuf_bytes_remaining)
    return nc, P


_CACHE = {}


def _prep_weights(inp, depth):
    f = lambda a: np.ascontiguousarray(np.asarray(a, dtype=np.float32))
    m = {
        "w1": f(inp["ffn_w1"][:depth]).reshape(depth * 2 * D, DFF),
        "w3": f(inp["ffn_w3"][:depth]).reshape(depth * 2 * D, DFF),
        "w2": f(inp["ffn_w2"][:depth]).reshape(depth * 2 * DFF, D),
        "win": f(inp["w_in"][:depth]).reshape(depth * D, INC),
        "wpa": f(inp["w_pa"][:depth]).reshape(depth * 512, D),
        "wpb": f(inp["w_pb"][:depth]).reshape(depth * D, D),
        "wo": f(inp["w_o"][:depth]).reshape(depth * D, D),
        "lng": f(inp["ln_g"][:depth]).reshape(depth * 24, 128),
        "lnb": f(inp["ln_b"][:depth]).reshape(depth * 24, 128),
        "gng": f(inp["gla_norm_g"][:depth]).reshape(depth * 2, 128),
        "gmg": f(inp["gm_ln_g"][:depth]), "gmb": f(inp["gm_ln_b"][:depth]),
        "bs": f(inp["gm_bs"][:depth]).reshape(depth, 512),
        "ba": f(inp["gla_ba"][:depth]),
        "ws": f(inp["gm_ws"][:depth]).reshape(depth * 4, 128, 128),
        "wa2": f(inp["gla_wa2"][:depth]),
    }
    return m


def run(inp, depth, nseq, seqlen, has_sample, ncores):
    key = (depth, nseq, seqlen, has_sample)
    if key not in _CACHE:
        _CACHE[key] = build(depth, nseq, seqlen, has_sample)
    nc, P = _CACHE[key]
    wm = _prep_weights(inp, depth)
    xp_all = np.asarray(inp["x_prompt"], dtype=np.float32)
    xs_all = np.asarray(inp["x_sample"], dtype=np.float32)
    st_all = np.asarray(inp["state_gla"], dtype=np.float32)
    in_maps = []
    for c in range(ncores):
        m = dict(wm)
        m["xp"] = np.ascontiguousarray(xp_all[c * nseq:(c + 1) * nseq, :seqlen]).reshape(nseq * seqlen, D)
        m["xs"] = np.ascontiguousarray(xs_all[c])
        m["st0"] = np.ascontiguousarray(st_all[:depth, c])
        in_maps.append(m)
    res = run_bass_kernel_spmd(nc, in_maps, core_ids=list(range(ncores)))
    R = res.results
    yp = np.concatenate([r["yp"].reshape(nseq, seqlen, D) for r in R], axis=0)
    ys = np.stack([r["ys"] for r in R], axis=0)
    spo = np.concatenate([r["sp_out"] for r in R], axis=1)
    sso = np.stack([r["ss_out"] for r in R], axis=1)
    vso = np.stack([r["vs_out"] for r in R], axis=1)
    return (yp.astype(np.float32), ys.astype(np.float32), spo.astype(np.float32), sso.astype(np.float32), vso.astype(np.float32))


def kernel(**inputs):
    return run(inputs, DEPTH, 2, 2048, True, 8)
```

```python
import numpy as np
import concourse.bass as bass
import concourse.mybir as mybir
from concourse.bass_utils import run_bass_kernel_spmd

F32 = mybir.dt.float32
BF16 = mybir.dt.bfloat16
AF = mybir.ActivationFunctionType
ALU = mybir.AluOpType

D = 1024
DFF = 2816
NF = DFF // 128
INC = 6160
O_ZU, O_ZV, O_Q, O_K, O_V, O_GG, O_GLR, O_GA, O_GB = 0, 512, 1024, 1536, 2048, 3072, 4096, 4112, 5136
DEPTH = 4
ALPHA = (2.0 * DEPTH) ** 0.25
EPS = 1e-5
TT = 512
N_FILL = 22


def I(name, *a, **kw):
    return (name, a, kw)


def _call(eng, d):
    return getattr(eng, d[0])(*d[1], **d[2])


class Buf:
    __slots__ = ("name", "w", "r")

    def __init__(self, name):
        self.name = name
        self.w = None
        self.r = {}


class Prog:
    ENGS = ("pe", "act", "dve", "pool", "sp")

    def __init__(self, nc):
        self.nc = nc
        self.sem = {}
        self.cnt = {}
        self.waited = {e: {} for e in self.ENGS}
        for e in self.ENGS:
            self.sem[e] = nc.alloc_semaphore(name="s_" + e)
            self.cnt[e] = 0
        self.q = {e: [] for e in self.ENGS}
        self.n_inst = 0

    def dma_sem(self, key):
        sk = ("dma", key)
        if sk not in self.sem:
            self.sem[sk] = self.nc.alloc_semaphore(name="d_" + str(key))
            self.cnt[sk] = 0
        return sk

    def _need(self, e, semkey, val):
        w = self.waited[e]
        if w.get(semkey, 0) >= val:
            return
        sem = self.sem[semkey]
        self.q[e].append(lambda eng, sem=sem, val=val: eng.wait_ge(sem, val))
        w[semkey] = val

    def _deps(self, e, reads, writes, raw_same=True):
        for b in reads:
            if b.w is not None:
                k, v = b.w
                if k == e and not raw_same:
                    continue
                self._need(e, k, v)
        for b in writes:
            if b.w is not None:
                k, v = b.w
                if k != e:
                    self._need(e, k, v)
            for k, v in b.r.items():
                if k != e:
                    self._need(e, k, v)

    def _mark(self, k, v, reads, writes):
        for b in reads:
            if b.r.get(k, 0) < v:
                b.r[k] = v
        for b in writes:
            b.w = (k, v)
            b.r = {}

    def op(self, e, fn, reads=(), writes=()):
        self._deps(e, reads, writes)
        sem = self.sem[e]
        self.q[e].append(lambda eng, fn=fn, sem=sem: _call(eng, fn).then_inc(sem, 1))
        self.cnt[e] += 1
        self._mark(e, self.cnt[e], reads, writes)
        self.n_inst += 1

    def mm(self, fns, reads=(), writes=(), per=None):
        e = "pe"
        self._deps(e, reads, writes, raw_same=False)
        sem = self.sem[e]
        n = len(fns)
        for i, fn in enumerate(fns):
            if per is not None:
                self._deps(e, per[i], (), raw_same=False)
            if i < n - 1:
                self.q[e].append(lambda eng, fn=fn: _call(eng, fn))
            else:
                self.q[e].append(lambda eng, fn=fn, sem=sem: _call(eng, fn).then_inc(sem, 1))
        self.cnt[e] += 1
        allreads = list(reads)
        if per is not None:
            for p_ in per:
                allreads.extend(p_)
        self._mark(e, self.cnt[e], allreads, writes)
        self.n_inst += n

    def dma(self, q, key, out, in_, reads=(), writes=()):
        sk = self.dma_sem(key)
        self._deps(q, reads, writes)
        sem = self.sem[sk]
        self.q[q].append(lambda eng, out=out, in_=in_, sem=sem: eng.dma_start(out=out, in_=in_).then_inc(sem, 16))
        self.cnt[sk] += 16
        self._mark(sk, self.cnt[sk], reads, writes)
        self.n_inst += 1

    def barrier(self, engs=("pe", "act", "dve", "pool")):
        for e in engs:
            for o in engs:
                if o != e and self.cnt[o] > 0:
                    self._need(e, o, self.cnt[o])

    def finish(self):
        for k, v in self.cnt.items():
            if v > 0 and k != "sp":
                self._need("sp", k, v)

    def run(self):
        nc = self.nc
        with nc.Block() as block:
            def mk(e):
                def f(eng):
                    for t in self.q[e]:
                        t(eng)
                return f
            block.tensor(mk("pe"))
            block.scalar(mk("act"))
            block.vector(mk("dve"))
            block.gpsimd(mk("pool"))
            block.sync(mk("sp"))


def build(depth, nseq, seqlen, has_sample, n_w_slots=5):
    nc = bass.Bass("TRN2", target_bir_lowering=False)
    P = Prog(nc)
    NTOK = nseq * seqlen

    def din(name, shape, dt=F32):
        return nc.dram_tensor(name, shape, dt, kind="ExternalInput").ap()

    def dout(name, shape):
        return nc.dram_tensor(name, shape, F32, kind="ExternalOutput").ap()

    xp = din("xp", [max(NTOK, 1), D])
    xs = din("xs", [64, D])
    st0 = din("st0", [depth, 4, 128, 256])
    w_f32 = {
        "w1": din("w1", [depth * 2 * D, DFF]), "w3": din("w3", [depth * 2 * D, DFF]),
        "w2": din("w2", [depth * 2 * DFF, D]), "win": din("win", [depth * D, INC]),
        "wpa": din("wpa", [depth * 512, D]), "wpb": din("wpb", [depth * D, D]), "wo": din("wo", [depth * D, D]),
    }
    lng_d = din("lng", [depth * 24, 128])
    lnb_d = din("lnb", [depth * 24, 128])
    gng_d = din("gng", [depth * 2, 128])
    gmg_d = din("gmg", [depth, 512])
    gmb_d = din("gmb", [depth, 512])
    bs_d = din("bs", [depth, 512])
    ba_d = din("ba", [depth, 512])
    ws_d = din("ws", [depth * 4, 128, 128])
    wa2_d = din("wa2", [depth, 16, 512])

    yp = dout("yp", [max(NTOK, 1), D])
    ys = dout("ys", [64, D])
    sp_out = dout("sp_out", [depth, max(nseq, 1), 4, 128, 256])
    ss_out = dout("ss_out", [depth, 4, 128, 256])
    vs_out = dout("vs_out", [depth, 64, 512])

    w_bf = {k: nc.dram_tensor(k + "_bf", list(v.shape), BF16).ap() for k, v in w_f32.items()}
    w_rows = {"w1": 2 * D, "w3": 2 * D, "w2": 2 * DFF, "win": D, "wpa": 512, "wpb": D, "wo": D}
    scrB = {(l, g): Buf("scr%d_%d" % (l, g)) for l in range(depth) for g in range(3)}
    GRP = {"w1": None, "w3": None, "w2": None, "win": 1, "wpa": 1, "wpb": 1, "wo": 1}

    def emit_conversions():
        for l in range(depth):
            for g in range(3):
                if g == 1:
                    items = [("win", l * D, D), ("wpa", l * 512, 512), ("wpb", l * D, D), ("wo", l * D, D)]
                else:
                    sidx = 0 if g == 0 else 1
                    items = [("w1", (l * 2 + sidx) * D, D), ("w3", (l * 2 + sidx) * D, D), ("w2", (l * 2 + sidx) * DFF, DFF)]
                for (k, r0, nr) in items:
                    step = 256
                    for a in range(0, nr, step):
                        b = min(nr, a + step)
                        P.dma("pool", "cvt%d_%d" % (l, g), w_bf[k][r0 + a:r0 + b, :], w_f32[k][r0 + a:r0 + b, :], writes=[scrB[(l, g)]])

    def sb(name, shape, dt=F32):
        return nc.alloc_sbuf_tensor(name, shape, dt)

    ident = sb("ident", [128, 128]); onesf = sb("onesf", [128, 128])
    ucum = sb("ucum", [128, 128]); lst = sb("lst", [128, 128])
    maskb = sb("maskb", [128, 128], BF16)
    ones_ln = sb("ones_ln", [128, 128], BF16); ones_rms = sb("ones_rms", [128, 128], BF16)
    tmpc = sb("tmpc", [128, 128])
    Bc = Buf("consts")
    P.op("pool", I("memset", onesf[:], 1.0), writes=[Bc])
    P.op("pool", I("memset", ones_ln[:], 1.0 / D), writes=[Bc])
    P.op("pool", I("memset", ones_rms[:], 1.0 / 256), writes=[Bc])
    P.op("pool", I("memset", tmpc[:], -1.0 / 16), writes=[Bc])
    dmy = sb("dmy", [128, 2]); Bdmy = Buf("dmy")
    eps_ln_t = sb("eps_ln_t", [128, 2])
    P.op("pool", I("memset", eps_ln_t[:, 0:1], EPS / (ALPHA * ALPHA)), writes=[Bc])
    P.op("pool", I("memset", eps_ln_t[:, 1:2], EPS), writes=[Bc])
    P.op("pool", I("memset", dmy[:], 1.0), writes=[Bc])

    def preload_ln_table():
        P.op("act", I("activation", out=dmy[:, 1:2], in_=dmy[:, 0:1], func=AF.Ln), reads=[Bc], writes=[Bdmy])
    P.op("pool", I("affine_select", out=ident[:], in_=onesf[:], pattern=[[-1, 128]], compare_op=ALU.is_equal, fill=0.0, base=0, channel_multiplier=1), reads=[Bc], writes=[Bc])
    P.op("pool", I("affine_select", out=ucum[:], in_=tmpc[:], pattern=[[1, 128]], compare_op=ALU.is_ge, fill=0.0, base=0, channel_multiplier=-1), reads=[Bc], writes=[Bc])
    P.op("pool", I("affine_select", out=lst[:], in_=tmpc[:], pattern=[[-1, 128]], compare_op=ALU.is_gt, fill=0.0, base=0, channel_multiplier=1), reads=[Bc], writes=[Bc])
    P.op("pool", I("affine_select", out=maskb[:], in_=onesf[:], pattern=[[1, 128]], compare_op=ALU.is_ge, fill=0.0, base=0, channel_multiplier=-1), reads=[Bc], writes=[Bc])

    banks = [(nc.alloc_psum_tensor("bank%d" % i, [128, 512], F32), Buf("bank%d" % i)) for i in range(8)]
    bank_i = [0]

    def nbank():
        b = banks[bank_i[0] % 6]
        bank_i[0] += 1
        return b

    lngT = sb("lngT", [128, depth * 24]); lnbT = sb("lnbT", [128, depth * 24]); gngT = sb("gngT", [128, depth * 2])
    stage = sb("stage", [128, 128])
    Bst = Buf("stage")
    for (src, dst, n) in ((lng_d, lngT, depth * 24), (lnb_d, lnbT, depth * 24), (gng_d, gngT, depth * 2)):
        P.dma("sp", "misc", stage[0:n, :], src[:, :], writes=[Bst])
        ps, Bp = nbank()
        P.mm([I("transpose", ps[:, 0:n], stage[0:n, :], ident[0:n, 0:n])], reads=[Bst, Bc], writes=[Bp])
        P.op("dve", I("tensor_copy", out=dst[:, 0:n], in_=ps[:, 0:n]), writes=[Bp, Bc])
    WT = sb("WT", [128, depth * 4, 128], BF16)
    wstage = sb("wstage", [128, 128])
    for i in range(depth * 4):
        P.dma("sp", "misc", stage[:, :], ws_d[i], writes=[Bst])
        Bw = Buf("wst")
        P.op("pool", I("affine_select", out=wstage[:], in_=stage[:], pattern=[[-1, 128]], compare_op=ALU.is_ge, fill=0.0, base=0, channel_multiplier=1), reads=[Bst], writes=[Bw])
        ps, Bp = nbank()
        P.mm([I("transpose", ps[:, 0:128], wstage[:, :], ident[:, :])], reads=[Bw, Bc], writes=[Bp])
        P.op("dve", I("tensor_copy", out=WT[:, i, :], in_=ps[:, 0:128]), reads=[], writes=[Bp, Bc, Bst])
    wa2b = sb("wa2b", [16, depth, 512], BF16)
    for l in range(depth):
        P.dma("pool", "misc2", wa2b[:, l, :], wa2_d[l], writes=[Bc])

    S = [sb("S%d" % l, [128, 4, 256]) for l in range(depth)]
    BS = [Buf("S%d" % l) for l in range(depth)]
    Sbf = sb("Sbf", [128, 4, 256], BF16); BSbf = Buf("Sbf")
    xres = sb("xres", [128, 8, TT]); xbf = sb("xbf", [128, 8, TT], BF16)
    Bxr = [Buf("xr%d" % c) for c in range(8)]
    Bxb = [Buf("xb%d" % c) for c in range(8)]
    xio = sb("xio", [128, D]); Bxio = Buf("xio")
    vnf = sb("vnf", [128, 512]); Bvnf = Buf("vnf")
    lc = sb("lc", [128, 4, 512]); Blc = Buf("lc")

    direct = [True]
    blockB = {}
    pool_ok = [False]

    def PL(alt="dve"):
        return "pool" if pool_ok[0] else alt

    slots = [(sb("wslot%d" % i, [128, 4096], BF16), Buf("wslot%d" % i)) for i in range(n_w_slots)]
    slot_i = [0]

    def load_w(key, l, r0, nk, c0, ncols, g=1):
        def last_use(si):
            Bs = slots[si][1]
            if Bs.w is None:
                return -1
            return Bs.r.get("pe", 1 << 60)
        si = min(range(n_w_slots), key=lambda i_: (last_use(i_), (i_ - slot_i[0]) % n_w_slots))
        slot_i[0] = si + 1
        t, B = slots[si]
        v = t[:, 0:nk * ncols].rearrange("p (k n) -> p k n", k=nk)
        src = w_bf[key][r0:r0 + nk * 128, c0:c0 + ncols].rearrange("(k p) n -> p k n", p=128)
        blk = (key, r0, nk, c0, ncols)
        if direct[0]:
            src32 = w_f32[key][r0:r0 + nk * 128, c0:c0 + ncols].rearrange("(k p) n -> p k n", p=128)
            P.dma("pool", "ws%d" % si, v, src32, writes=[B])
            if blk not in blockB:
                blockB[blk] = Buf("blk")
            P.dma("sp", "wb%d" % si, src, v, reads=[B], writes=[blockB[blk]])
        else:
            P.dma("sp", "ws%d" % si, v, src, reads=[blockB[blk]], writes=[B])
        return v, B

    RN = 40700
    R = sb("R", [128, RN], BF16)

    class Carver:
        def __init__(self):
            self.off = 0

        def take(self, n_elems, dt, shape):
            nb = n_elems * (4 if dt == F32 else 2)
            nb = (nb + 31) // 32 * 32
            a = self.off
            self.off += nb // 2
            assert self.off <= RN, ("R overflow", self.off)
            v = R[:, a:a + nb // 2]
            if dt == F32:
                v = v.bitcast(F32)
            v = v[:, 0:n_elems]
            if len(shape) == 3:
                v = v.rearrange("p (a b) -> p a b", a=shape[1])
            return v

    CY_F = 0.5 / ALPHA
    CY_M = 1.0 / ALPHA
    EPS_LN = EPS / (ALPHA * ALPHA)

    ln_sqb = sb("ln_sqb", [128, 8, TT], BF16); Bln_sq = [Buf("lsq%d" % c) for c in range(8)]
    ln_mean = sb("ln_mean", [128, TT]); Bln_mean = Buf("lmean")
    ln_msq = sb("ln_msq", [128, TT]); Bln_msq = Buf("lmsq")
    ln_var = sb("ln_var", [128, TT]); Bln_var = Buf("lvar")
    ln_rstd = sb("ln_rstd", [128, TT]); Bln_rstd = Buf("lrstd")
    pm, Bpm = banks[6]
    pq, Bpq = banks[7]

    def ln_chunk_prep(c, T):
        P.op("act", I("activation", out=ln_sqb[:, c, :T], in_=xres[:, c, :T], func=AF.Square), reads=[Bxr[c]], writes=[Bln_sq[c]])
        P.op(PL(), I("tensor_copy", out=xbf[:, c, :T], in_=xres[:, c, :T]), reads=[Bxr[c]], writes=[Bxb[c]])

    def ln_chunk_stats(c, T):
        P.mm([I("matmul", pm[:, :T], lhsT=ones_ln[:, :], rhs=xbf[:, c, :T], start=(c == 0), stop=(c == 7))], reads=[Bxb[c], Bc], writes=[Bpm])
        P.mm([I("matmul", pq[:, :T], lhsT=ones_ln[:, :], rhs=ln_sqb[:, c, :T], start=(c == 0), stop=(c == 7))], reads=[Bln_sq[c], Bc], writes=[Bpq])

    def layer_norm(l, s, T):
        kw_ps, kw_B = nbank()

        def keep_warm(after):
            P.mm([I("matmul", kw_ps[:, 0:16], lhsT=ones_ln[:, :], rhs=ones_ln[:, 0:16], start=True, stop=True)], reads=[after, Bc], writes=[kw_B])

        P.mm([I("matmul", kw_ps[:, :T], lhsT=ones_ln[:, :], rhs=ln_sqb[:, 7, :T], start=True, stop=True) for _ in range(N_FILL)],
             reads=[Bln_sq[7], Bc], writes=[kw_B])
        P.op("act", I("activation", out=ln_msq[:, :T], in_=pm[:, :T], func=AF.Square), writes=[Bln_msq, Bpm])
        P.op("act", I("activation", out=ln_mean[:, :T], in_=pm[:, :T], func=AF.Copy), writes=[Bln_mean, Bpm])
        keep_warm(Bln_mean)
        P.op("dve", I("tensor_tensor", out=ln_var[:, :T], in0=pq[:, :T], in1=ln_msq[:, :T], op=ALU.subtract), reads=[Bln_msq], writes=[Bln_var, Bpq])
        P.op("act", I("activation", out=ln_var[:, :T], in_=ln_var[:, :T], func=AF.Ln, bias=eps_ln_t[:, 0:1], scale=1.0), reads=[Bln_var, Bc], writes=[Bln_var])
        keep_warm(Bln_var)
        P.op("act", I("activation", out=ln_rstd[:, :T], in_=ln_var[:, :T], func=AF.Exp, scale=-0.5), reads=[Bln_var], writes=[Bln_rstd])
        keep_warm(Bln_rstd)
        col = (l * 3 + s) * 8
        gcs = [lngT[:, col + c:col + c + 1] for c in range(8)]
        bcs = [lnbT[:, col + c:col + c + 1] for c in range(8)]

        def sub(c):
            P.op("dve", I("tensor_tensor", out=xres[:, c, :T], in0=xres[:, c, :T], in1=ln_mean[:, :T], op=ALU.subtract), reads=[Bxr[c], Bln_mean], writes=[Bxr[c]])

        def mul(c):
            P.op("dve", I("tensor_tensor", out=xres[:, c, :T], in0=xres[:, c, :T], in1=ln_rstd[:, :T], op=ALU.mult), reads=[Bxr[c], Bln_rstd], writes=[Bxr[c]])
            P.op("act", I("activation", out=xbf[:, c, :T], in_=xres[:, c, :T], func=AF.Identity, scale=gcs[c], bias=bcs[c]), reads=[Bxr[c], Bc], writes=[Bxb[c]])

        for c in range(4):
            sub(c)
        for c in range(4):
            mul(c)
            sub(c + 4)
        for c in range(4, 8):
            mul(c)
        for c in range(8):
            P.op("dve", I("tensor_scalar", out=xres[:, c, :T], in0=xres[:, c, :T], scalar1=gcs[c], scalar2=bcs[c], op0=ALU.mult, op1=ALU.add), reads=[Bxr[c], Bc], writes=[Bxr[c]])

    Bh = [Buf("h%d" % j) for j in range(NF)]
    Bsil = [Buf("sil0"), Buf("sil1"), Buf("sil2")]

    def ffn(l, s, T):
        cv = Carver()
        hT = cv.take(NF * TT, BF16, [128, NF, TT])
        sil = [cv.take(TT, F32, [128, TT]) for _ in range(3)]
        r0 = (l * 2 + s) * D
        for blk in range(0, NF, 4):
            nch = min(4, NF - blk)
            w1v, B1 = load_w("w1", l, r0, 8, blk * 128, nch * 128, g=2 * s)
            w3v, B3 = load_w("w3", l, r0, 8, blk * 128, nch * 128, g=2 * s)
            for jj in range(nch):
                j = blk + jj
                pa, Bpa = nbank()
                P.mm([I("matmul", pa[:, :T], lhsT=w1v[:, k, jj * 128:(jj + 1) * 128], rhs=xbf[:, k, :T], start=(k == 0), stop=(k == 7)) for k in range(8)], reads=[B1], per=[[Bxb[k]] for k in range(8)], writes=[Bpa])
                pb, Bpb = nbank()
                P.mm([I("matmul", pb[:, :T], lhsT=w3v[:, k, jj * 128:(jj + 1) * 128], rhs=xbf[:, k, :T], start=(k == 0), stop=(k == 7)) for k in range(8)], reads=[B3], per=[[Bxb[k]] for k in range(8)], writes=[Bpb])
                st, Bs_ = sil[j % 3], Bsil[j % 3]
                P.op("act", I("activation", out=st[:, :T], in_=pa[:, :T], func=AF.Silu), writes=[Bs_, Bpa])
                P.op("dve", I("tensor_tensor", out=hT[:, j, :T], in0=pb[:, :T], in1=st[:, :T], op=ALU.mult), reads=[Bs_], writes=[Bh[j], Bpb])
        r2 = (l * 2 + s) * DFF
        preload_ln_table()
        for c in range(8):
            halves = []
            for (k0, nk) in ((0, 16), (16, 6)):
                halves.append(load_w("w2", l, r2 + k0 * 128, nk, c * 128, 128, g=2 * s))
            py, Bpy = nbank()
            fns = []
            per = []
            for hi, (k0, nk) in enumerate(((0, 16), (16, 6))):
                wv = halves[hi][0]
                for k in range(nk):
                    fns.append(I("matmul", py[:, :T], lhsT=wv[:, k, :], rhs=hT[:, k0 + k, :T], start=(k0 + k == 0), stop=(k0 + k == NF - 1)))
                    per.append([Bh[k0 + k], halves[hi][1]])
            P.mm(fns, reads=[], writes=[Bpy], per=per)
            P.op("dve", I("scalar_tensor_tensor", out=xres[:, c, :T], in0=py[:, :T], scalar=CY_F, in1=xres[:, c, :T], op0=ALU.mult, op1=ALU.add), reads=[Bxr[c]], writes=[Bxr[c], Bpy])
            ln_chunk_prep(c, T)
            if c > 0:
                ln_chunk_stats(c - 1, T)
        ln_chunk_stats(7, T)

    def mixer(l, T, first, s0_ap, s_out_ap, v_out_ap):
        C = min(T, 128)
        NCH = T // C
        cv = Carver()
        uT = cv.take(4 * TT, BF16, [128, 4, TT]); Bu = [Buf("u%d" % g) for g in range(4)]
        gv = [cv.take(512, F32, [128, 512]) for _ in range(2)] + [xio[:, 0:512], xio[:, 512:1024]]
        Bgv = [Buf("gv0"), Buf("gv1"), Bxio, Bxio]
        vnb = cv.take(4 * 512, BF16, [128, 4, 512]); Bvnb = [Buf("vnb%d" % i) for i in range(4)]
        stt = cv.take(32, F32, [128, 4, 8]); Bstt = [Buf("stt%d" % i) for i in range(4)]
        mv = cv.take(16, F32, [128, 4, 4]); Bmv = Buf("mv")
        qk = cv.take(8 * TT, BF16, [128, 8, TT]); Bq = Buf("qT"); Bk = Buf("kT")
        qT = qk[:, 0:4, :]; kT = qk[:, 4:8, :]; mT = qk
        sgg = cv.take(8 * TT, BF16, [128, 8, TT]); Bsgg = [Buf("sgg%d" % i) for i in range(4)]
        glrT = cv.take(TT, BF16, [128, TT]); Bglr = Buf("glr")
        vtok = cv.take(4 * 1024, BF16, [128, 4, 1024]); Bvt = [Buf("vt%d" % i) for i in range(4)]
        kd = cv.take(4 * 512, BF16, [128, 4, 512]); Bkd = [Buf("kd%d" % i) for i in range(4)]
        lgp = [cv.take(512, F32, [128, 512]) for _ in range(2)]; Blg = [Buf("lgp0"), Buf("lgp1")]
        ekd = [cv.take(512, F32, [128, 512]) for _ in range(2)]; Bekd = [Buf("ekd0"), Buf("ekd1")]
        eb = [cv.take(512, F32, [128, 4, 128]) for _ in range(2)]; Beb = [Buf("eb0"), Buf("eb1")]
        enb = [cv.take(512, F32, [128, 4, 128]) for _ in range(2)]; Benb = [Buf("enb0"), Buf("enb1")]
        ebl = [cv.take(16, F32, [128, 4, 4]) for _ in range(2)]; Bebl = [Buf("ebl0"), Buf("ebl1")]
        qe = [cv.take(512, BF16, [128, 4, 128]) for _ in range(2)]; Bqe = [Buf("qe0"), Buf("qe1")]
        ke = [cv.take(512, BF16, [128, 4, 128]) for _ in range(2)]; Bke = [Buf("ke0"), Buf("ke1")]
        attm = [cv.take(512, BF16, [128, 4, 128]) for _ in range(2)]; Batt = [Buf("att0"), Buf("att1")]
        sq = [cv.take(1024, BF16, [128, 8, 128]) for _ in range(2)]; Bsq = [Buf("osq0"), Buf("osq1")]
        rsd = [cv.take(512, F32, [128, 4, 128]) for _ in range(2)]; Brsd = [Buf("rsd0"), Buf("rsd1")]
        spt = rsd[0]; Bspt = Brsd[0]
        otmp = [cv.take(1024, F32, [128, 8, 128]) for _ in range(2)]; Bot = [Buf("ot0"), Buf("ot1")]
        sga = vnf; Bsga = Bvnf
        m1s = gv; Bm1s = Bgv
        Bm = [Buf("m%d" % c) for c in range(8)]
        assert cv.off <= RN, cv.off
        r0 = l * D

        for i, src in enumerate((gmg_d, gmb_d, bs_d, ba_d)):
            P.dma("sp", "lc", lc[:, i, :], src[l:l + 1, :].partition_broadcast(128), writes=[Blc])

        def proj_fm(c0, nchunks, evac):
            for b0 in range(0, nchunks, 4):
                nb = min(4, nchunks - b0)
                wv, Bw = load_w("win", l, r0, 8, c0 + b0 * 128, nb * 128)
                for jj in range(nb):
                    ps, Bp = nbank()
                    P.mm([I("matmul", ps[:, :T], lhsT=wv[:, k, jj * 128:(jj + 1) * 128], rhs=xbf[:, k, :T], start=(k == 0), stop=(k == 7)) for k in range(8)], reads=[Bw], per=[[Bxb[k]] for k in range(8)], writes=[Bp])
                    evac(b0 + jj, ps, Bp)

        proj_fm(O_ZU, 4, lambda j, ps, Bp: P.op("act", I("activation", out=uT[:, j, :T], in_=ps[:, :T], func=AF.Gelu), writes=[Bu[j], Bp]))
        wzv, Bwzv = load_w("win", l, r0, 8, O_ZV, 512)
        def zv1(i):
            ps, Bp = nbank()
            P.mm([I("matmul", ps[:C, :], lhsT=xbf[:, k, i * C:(i + 1) * C], rhs=wzv[:, k, :], start=(k == 0), stop=(k == 7)) for k in range(8)], reads=[Bwzv], writes=[Bp], per=[[Bxb[k]] for k in range(8)])
            g_ = gv[i]; Bg_ = Bgv[i]; st_ = stt[:, i, :]; mv_ = mv[:, i, :]
            P.op("act", I("activation", out=g_[:C, :], in_=ps[:C, :], func=AF.Gelu), writes=[Bg_, Bp])
            P.op("dve", I("bn_stats", out=st_[:C, 0:6], in_=g_[:C, :]), reads=[Bg_], writes=[Bstt[i]])
            P.op("dve", I("bn_aggr", out=mv_[:C, 0:2], in_=st_[:C, 0:6]), reads=[Bstt[i]], writes=[Bmv])
            P.op("dve", I("tensor_scalar", out=mv_[:C, 2:3], in0=mv_[:C, 1:2], scalar1=EPS, scalar2=None, op0=ALU.add), reads=[Bmv], writes=[Bmv])

        def zv2(i):
            g_ = gv[i]; Bg_ = Bgv[i]; mv_ = mv[:, i, :]
            P.op("dve", I("tensor_scalar", out=g_[:C, :], in0=g_[:C, :], scalar1=mv_[:C, 0:1], scalar2=mv_[:C, 3:4], op0=ALU.subtract, op1=ALU.mult), reads=[Bg_, Bmv], writes=[Bg_])
            P.op("dve", I("tensor_tensor", out=g_[:C, :], in0=g_[:C, :], in1=lc[:C, 0, :], op=ALU.mult), reads=[Bg_, Blc], writes=[Bg_])
            if v_out_ap is not None:
                P.op("dve", I("tensor_tensor", out=vnf[:C, :], in0=g_[:C, :], in1=lc[:C, 1, :], op=ALU.add), reads=[Bg_, Blc], writes=[Bvnf])
                P.dma("sp", "vout", v_out_ap, vnf[:C, :], reads=[Bvnf])
            P.op("dve", I("tensor_tensor", out=vnb[:C, i, :], in0=g_[:C, :], in1=lc[:C, 1, :], op=ALU.add), reads=[Bg_, Blc], writes=[Bvnb[i]])

        for i in range(NCH):
            zv1(i)
        proj_fm(O_Q, 4, lambda j, ps, Bp: P.op("act", I("activation", out=qT[:, j, :T], in_=ps[:, :T], func=AF.Copy, scale=128.0 ** -0.5), writes=[Bq, Bp]))
        P.op("act", I("activation", out=mv[:C, 0:NCH, 2:3], in_=mv[:C, 0:NCH, 2:3], func=AF.Sqrt), reads=[Bmv], writes=[Bmv])
        P.op("dve", I("reciprocal", out=mv[:C, 0:NCH, 3:4], in_=mv[:C, 0:NCH, 2:3]), reads=[Bmv], writes=[Bmv])
        proj_fm(O_K, 4, lambda j, ps, Bp: P.op("act", I("activation", out=kT[:, j, :T], in_=ps[:, :T], func=AF.Copy), writes=[Bk, Bp]))
        for i in range(NCH):
            zv2(i)
        wgl, Bwgl = load_w("win", l, r0, 8, O_GLR, 16)
        ps, Bp = nbank()
        P.mm([I("matmul", ps[:16, :T], lhsT=wgl[:, k, :], rhs=xbf[:, k, :T], start=(k == 0), stop=(k == 7)) for k in range(8)], reads=[Bwgl], per=[[Bxb[k]] for k in range(8)], writes=[Bp])
        P.op("act", I("activation", out=glrT[:16, :T], in_=ps[:16, :T], func=AF.Copy), writes=[Bglr, Bp])

        fillers = []
        gg_state = {}

        def gg_task(j):
            def run():
                blk = j // 4
                if blk not in gg_state:
                    gg_state[blk] = load_w("win", l, r0, 8, O_GG + blk * 512, 512)
                wv, Bw = gg_state[blk]
                jj = j % 4
                ps, Bp = nbank()
                P.mm([I("matmul", ps[:, :T], lhsT=wv[:, k, jj * 128:(jj + 1) * 128], rhs=xbf[:, k, :T], start=(k == 0), stop=(k == 7)) for k in range(8)], reads=[Bw], per=[[Bxb[k]] for k in range(8)], writes=[Bp])
                P.op("act", I("activation", out=sgg[:, j, :T], in_=ps[:, :T], func=AF.Silu), writes=Bsgg + [Bp])
            return run

        def spatial_task(i):
            def run():
                ps, Bp = nbank()
                for g in range(4):
                    P.mm([I("matmul", ps[:, g * C:(g + 1) * C], lhsT=vnb[:C, i, g * 128:(g + 1) * 128], rhs=WT[:C, l * 4 + g, :C], start=True, stop=True)], reads=[Bvnb[i], Bc], writes=[Bp])
                pv = ps[:, 0:4 * C].rearrange("p (g c) -> p g c", g=4)
                P.op("dve", I("tensor_tensor", out=spt[:, :, :C], in0=pv, in1=lc[:, 2, :].rearrange("p (g c) -> p g c", g=4)[:, :, :C], op=ALU.add), reads=[Blc], writes=[Bspt, Bp])
                P.op("dve", I("tensor_tensor", out=uT[:, :, i * C:(i + 1) * C], in0=uT[:, :, i * C:(i + 1) * C], in1=spt[:, :, :C], op=ALU.mult), reads=[Bspt] + Bu, writes=Bu)
            return run

        mg_state = {}

        def merge_p1(hb, cc):
            def run():
                if hb not in mg_state:
                    mg_state[hb] = (load_w("wpa", l, l * 512, 4, hb * 512, 512), load_w("win", l, r0, 8, O_GA + hb * 512, 512))
                (wpa_v, Bwpa), (wga_v, Bwga) = mg_state[hb]
                cs = slice(cc * 128, (cc + 1) * 128)
                pg, Bpg = nbank()
                P.mm([I("matmul", pg[:, :T], lhsT=wga_v[:, k, cs], rhs=xbf[:, k, :T], start=(k == 0), stop=(k == 7)) for k in range(8)], reads=[Bwga], per=[[Bxb[k]] for k in range(8)], writes=[Bpg])
                pp, Bpp = nbank()
                P.mm([I("matmul", pp[:, :T], lhsT=wpa_v[:, k, cs], rhs=uT[:, k, :T], start=(k == 0), stop=(k == 3)) for k in range(4)], reads=[Bwpa], writes=[Bpp], per=[[Bu[k]] for k in range(4)])
                P.op("act", I("activation", out=sga[:, :T], in_=pg[:, :T], func=AF.Sigmoid), writes=[Bsga, Bpg])
                P.op("dve", I("tensor_tensor", out=m1s[cc][:, :T], in0=pp[:, :T], in1=sga[:, :T], op=ALU.mult), reads=[Bsga], writes=[Bm1s[cc], Bpp])
            return run

        fill_plan = {}
        fill_plan.setdefault(0, []).extend(gg_task(j) for j in range(4))
        fill_plan.setdefault(1, []).extend(gg_task(j) for j in range(4, 8))
        fill_plan.setdefault(2, []).extend(spatial_task(i) for i in range(NCH))
        fill_plan.setdefault(NCH + 1, []).extend(merge_p1(0, cc) for cc in range(2))
        fill_plan.setdefault(NCH + 2, []).extend(merge_p1(0, cc) for cc in range(2, 4))
        pending_tasks = []
        if first:
            if s0_ap is None:
                P.op(PL(), I("memset", S[l][:], 0.0), writes=[BS[l]])
            else:
                P.dma("sp", "s0", S[l][:], s0_ap.rearrange("h k v -> k h v"), writes=[BS[l]])
        P.op("act", I("activation", out=Sbf[:], in_=S[l][:], func=AF.Copy), reads=[BS[l]], writes=[BSbf])
        wk, Bwk = load_w("win", l, r0, 8, O_K, 512)
        wv0, Bwv0 = load_w("win", l, r0, 8, O_V, 512)
        wv1, Bwv1 = load_w("win", l, r0, 8, O_V + 512, 512)
        held = {}

        def stA(i):
            b2 = i % 2
            tok = slice(i * C, (i + 1) * C)
            for hv, (wv_, Bwv_) in enumerate(((wv0, Bwv0), (wv1, Bwv1))):
                ps, Bp = nbank()
                P.mm([I("matmul", ps[:C, :], lhsT=xbf[:, k, tok], rhs=wv_[:, k, :], start=(k == 0), stop=(k == 7)) for k in range(8)], reads=[Bwv_], per=[[Bxb[k]] for k in range(8)], writes=[Bp])
                if hv == 0:
                    P.op("act", I("activation", out=vtok[:C, i, 0:512], in_=ps[:C, :], func=AF.Copy), writes=[Bvt[i], Bp])
                else:
                    P.op("dve", I("tensor_copy", out=vtok[:C, i, 512:1024], in_=ps[:C, :]), writes=[Bvt[i], Bp])
            ps, Bp = nbank()
            P.mm([I("matmul", ps[:C, :], lhsT=glrT[:16, tok], rhs=wa2b[:, l, :], start=True, stop=True)], reads=[Bglr, Bc], writes=[Bp])
            P.op("dve", I("tensor_tensor", out=lgp[b2][:C, :], in0=ps[:C, :], in1=lc[:C, 3, :], op=ALU.add), reads=[Blc], writes=[Blg[b2], Bp])
            P.op("act", I("activation", out=lgp[b2][:C, :], in_=lgp[b2][:C, :], func=AF.Exp, scale=-1.0), reads=[Blg[b2]], writes=[Blg[b2]])
            P.op("act", I("activation", out=lgp[b2][:C, :], in_=lgp[b2][:C, :], func=AF.Ln, bias=1.0, scale=1.0), reads=[Blg[b2]], writes=[Blg[b2]])

        def stB(i):
            b2 = i % 2
            tok = slice(i * C, (i + 1) * C)
            ps, Bp = nbank()
            P.mm([I("matmul", ps[:C, :], lhsT=lst[:C, :C], rhs=lgp[b2][:C, :], start=True, stop=True)], reads=[Blg[b2], Bc], writes=[Bp])
            P.op("act", I("activation", out=ekd[b2][:C, :], in_=ps[:C, :], func=AF.Exp), writes=[Bekd[b2], Bp])
            ps2, Bp2 = nbank()
            for h in range(4):
                P.mm([I("matmul", ps2[:, h * C:(h + 1) * C], lhsT=lgp[b2][:C, h * 128:(h + 1) * 128], rhs=ucum[:C, :C], start=True, stop=True)], reads=[Blg[b2], Bc], writes=[Bp2])
            pv = ps2[:, 0:4 * C].rearrange("p (h c) -> p h c", h=4)
            P.op("act", I("activation", out=eb[b2][:, :, :C], in_=pv, func=AF.Exp), writes=[Beb[b2], Bp2])
            P.op("act", I("activation", out=enb[b2][:, :, :C], in_=pv, func=AF.Exp, scale=-1.0), writes=[Benb[b2], Bp2])
            P.op("act", I("activation", out=ebl[b2][:, :, 0:1], in_=pv[:, :, C - 1:C], func=AF.Exp), writes=[Bebl[b2], Bp2])
            ps3, Bp3 = nbank()
            P.mm([I("matmul", ps3[:C, :], lhsT=xbf[:, k, tok], rhs=wk[:, k, :], start=(k == 0), stop=(k == 7)) for k in range(8)], reads=[Bwk], per=[[Bxb[k]] for k in range(8)], writes=[Bp3])
            P.op("dve", I("tensor_tensor", out=kd[:C, i, :], in0=ps3[:C, :], in1=ekd[b2][:C, :], op=ALU.mult), reads=[Bekd[b2]], writes=[Bkd[i], Bp3])
            P.op("dve", I("tensor_tensor", out=qe[b2][:, :, :C], in0=qT[:, :, tok], in1=eb[b2][:, :, :C], op=ALU.mult), reads=[Bq, Beb[b2]], writes=[Bqe[b2]])
            P.op(PL(), I("tensor_tensor", out=ke[b2][:, :, :C], in0=kT[:, :, tok], in1=enb[b2][:, :, :C], op=ALU.mult), reads=[Bk, Benb[b2]], writes=[Bke[b2]])

        def stC(i):
            b2 = i % 2
            ps, Bp = nbank()
            for h in range(4):
                P.mm([I("matmul", ps[:C, h * C:(h + 1) * C], lhsT=ke[b2][:, h, :C], rhs=qe[b2][:, h, :C], start=True, stop=True)], reads=[Bke[b2], Bqe[b2]], writes=[Bp])
            pv = ps[:C, 0:4 * C].rearrange("p (h c) -> p h c", h=4)
            P.op("dve", I("tensor_tensor", out=attm[b2][:C, :, :C], in0=pv, in1=maskb[:C, :C].unsqueeze(1).to_broadcast([C, 4, C]), op=ALU.mult), reads=[Bc], writes=[Batt[b2], Bp])

        def stD(i):
            b2 = i % 2
            po = [nbank(), nbank()]
            for j in range(8):
                h = j // 2
                pj, Bpj = po[j // 4]
                jc = (j % 4) * C
                P.mm([I("matmul", pj[:, jc:jc + C], lhsT=vtok[:C, i, j * 128:(j + 1) * 128], rhs=attm[b2][:C, h, :C], start=True, stop=False),
                      I("matmul", pj[:, jc:jc + C], lhsT=Sbf[:, h, (j % 2) * 128:(j % 2 + 1) * 128], rhs=qe[b2][:, h, :C], start=False, stop=True)],
                     reads=[Bvt[i], Batt[b2], BSbf, Bqe[b2]], writes=[Bpj])
            pS = [nbank(), nbank()]
            for h in range(4):
                pj, Bpj = pS[h // 2]
                P.mm([I("matmul", pj[:, (h % 2) * 256:(h % 2 + 1) * 256], lhsT=kd[:C, i, h * 128:(h + 1) * 128], rhs=vtok[:C, i, h * 256:(h + 1) * 256], start=True, stop=True)], reads=[Bkd[i], Bvt[i]], writes=[Bpj])
            for h in range(4):
                pj, Bpj = pS[h // 2]
                P.op("dve", I("scalar_tensor_tensor", out=S[l][:, h, :], in0=S[l][:, h, :], scalar=ebl[b2][:, h, 0:1], in1=pj[:, (h % 2) * 256:(h % 2 + 1) * 256], op0=ALU.mult, op1=ALU.add), reads=[Bebl[b2], BS[l]], writes=[BS[l], Bpj])
            P.op("act", I("activation", out=Sbf[:], in_=S[l][:], func=AF.Copy), reads=[BS[l]], writes=[BSbf])
            for hh in range(2):
                pj, Bpj = po[hh]
                pv = pj[:, 0:4 * C].rearrange("p (j c) -> p j c", j=4)
                P.op("act", I("activation", out=sq[b2][:, hh * 4:(hh + 1) * 4, :C], in_=pv, func=AF.Square), writes=[Bsq[b2], Bpj])
                P.op("act", I("activation", out=otmp[b2][:, hh * 4:(hh + 1) * 4, :C], in_=pv, func=AF.Copy), writes=[Bot[b2], Bpj])

        def stE(i):
            b2 = i % 2
            tok = slice(i * C, (i + 1) * C)
            pr, Bpr = nbank()
            for h in range(4):
                P.mm([I("matmul", pr[:, h * C:(h + 1) * C], lhsT=ones_rms[:, :], rhs=sq[b2][:, 2 * h, :C], start=True, stop=False),
                      I("matmul", pr[:, h * C:(h + 1) * C], lhsT=ones_rms[:, :], rhs=sq[b2][:, 2 * h + 1, :C], start=False, stop=True)], reads=[Bsq[b2], Bc], writes=[Bpr])
            pv = pr[:, 0:4 * C].rearrange("p (h c) -> p h c", h=4)
            P.op("act", I("activation", out=rsd[b2][:, :, :C], in_=pv, func=AF.Ln, bias=eps_ln_t[:, 1:2], scale=1.0), reads=[Bc], writes=[Brsd[b2], Bpr])
            P.op("act", I("activation", out=rsd[b2][:, :, :C], in_=rsd[b2][:, :, :C], func=AF.Exp, scale=-0.5), reads=[Brsd[b2]], writes=[Brsd[b2]])
            for par in range(2):
                ov = otmp[b2][:, :, :C].rearrange("p (h two) c -> p h two c", two=2)[:, :, par, :]
                P.op("dve", I("scalar_tensor_tensor", out=ov, in0=ov, scalar=gngT[:, l * 2 + par:l * 2 + par + 1], in1=rsd[b2][:, :, :C], op0=ALU.mult, op1=ALU.mult), reads=[Brsd[b2], Bot[b2], Bc], writes=[Bot[b2]])
            P.op(PL(), I("tensor_tensor", out=sgg[:, :, tok], in0=sgg[:, :, tok], in1=otmp[b2][:, :, :C], op=ALU.mult), reads=[Bot[b2], Bsgg[i]], writes=[Bsgg[i]])

        stages = [stA, stB, stC, stD, stE]
        for step in range(NCH + len(stages) - 1):
            for si in reversed(range(len(stages))):
                i = step - si
                if 0 <= i < NCH:
                    stages[si](i)
            for task in fill_plan.pop(step, ()):
                task()
        for k_ in sorted(fill_plan):
            pending_tasks.extend(fill_plan[k_])
        if s_out_ap is not None:
            P.dma("sp", "sout%d" % l, s_out_ap.rearrange("h k v -> k h v"), S[l][:], reads=[BS[l]])
        for t_ in pending_tasks:
            t_()
        del pending_tasks[:]
        for hb in range(2):
            if hb == 1:
                for cc in range(4):
                    merge_p1(1, cc)()
            wpb_v, Bwpb = load_w("wpb", l, l * D, 8, hb * 512, 512)
            wgb_v, Bwgb = load_w("win", l, r0, 8, O_GB + hb * 512, 512)
            for cc in range(4):
                c = hb * 4 + cc
                cs = slice(cc * 128, (cc + 1) * 128)
                pg2, Bpg2 = nbank()
                P.mm([I("matmul", pg2[:, :T], lhsT=wgb_v[:, k, cs], rhs=xbf[:, k, :T], start=(k == 0), stop=(k == 7)) for k in range(8)], reads=[Bwgb], per=[[Bxb[k]] for k in range(8)], writes=[Bpg2])
                pp2, Bpp2 = nbank()
                P.mm([I("matmul", pp2[:, :T], lhsT=wpb_v[:, k, cs], rhs=sgg[:, k, :T], start=(k == 0), stop=(k == 7)) for k in range(8)], reads=Bsgg + [Bwpb], writes=[Bpp2])
                P.op("act", I("activation", out=sga[:, :T], in_=pg2[:, :T], func=AF.Sigmoid), writes=[Bsga, Bpg2])
                P.op("dve", I("tensor_tensor", out=sga[:, :T], in0=pp2[:, :T], in1=sga[:, :T], op=ALU.mult), reads=[Bsga], writes=[Bsga, Bpp2])
                P.op("dve", I("tensor_tensor", out=mT[:, c, :T], in0=m1s[cc][:, :T], in1=sga[:, :T], op=ALU.add), reads=[Bm1s[cc], Bsga], writes=[Bm[c], Bq, Bk])
        preload_ln_table()
        for hb in range(2):
            wo_v, Bwo = load_w("wo", l, l * D, 8, hb * 512, 512)
            for cc in range(4):
                c = hb * 4 + cc
                cs = slice(cc * 128, (cc + 1) * 128)
                py, Bpy = nbank()
                P.mm([I("matmul", py[:, :T], lhsT=wo_v[:, k, cs], rhs=mT[:, k, :T], start=(k == 0), stop=(k == 7)) for k in range(8)], reads=[Bwo], writes=[Bpy], per=[[Bm[k]] for k in range(8)])
                P.op("dve", I("scalar_tensor_tensor", out=xres[:, c, :T], in0=py[:, :T], scalar=CY_M, in1=xres[:, c, :T], op0=ALU.mult, op1=ALU.add), reads=[Bxr[c]], writes=[Bxr[c], Bpy])
                ln_chunk_prep(c, T)
                if c > 0:
                    ln_chunk_stats(c - 1, T)
        ln_chunk_stats(7, T)

    XST0 = 16000
    xst = R[:, XST0:XST0 + 8192].bitcast(F32).rearrange("p (a b) -> p a b", a=4)
    Bxst = Buf("xst")

    def run_tile(T, x_src, y_dst, first, s0, s_out, v_out, prefetched=False, nxt=None):
        C = min(T, 128)
        for i in range(T // C):
            if prefetched:
                src_t, Bsrc = xst[:, i, :], Bxst
            else:
                P.dma("sp", "xin", xio[:C, :], x_src[i * C:(i + 1) * C, :], writes=[Bxio])
                src_t, Bsrc = xio, Bxio
            for half in range(2):
                ps, Bp = nbank()
                for cc in range(4):
                    c = half * 4 + cc
                    P.mm([I("transpose", ps[:, cc * C:(cc + 1) * C], src_t[:C, c * 128:(c + 1) * 128], ident[:C, :C])], reads=[Bsrc, Bc], writes=[Bp])
                pv = ps[:, 0:4 * C].rearrange("p (a b) -> p a b", a=4)
                P.op("act", I("activation", out=xres[:, half * 4:(half + 1) * 4, i * C:(i + 1) * C], in_=pv, func=AF.Copy), writes=Bxr[half * 4:(half + 1) * 4] + [Bp])
                P.op("dve", I("tensor_copy", out=xbf[:, half * 4:(half + 1) * 4, i * C:(i + 1) * C], in_=pv), writes=Bxb[half * 4:(half + 1) * 4] + [Bp])
        for l in range(depth):
            ffn(l, 0, T)
            layer_norm(l, 0, T)
            mixer(l, T, first, None if s0 is None else s0[l], None if s_out is None else s_out[l], None if v_out is None else v_out[l])
            layer_norm(l, 1, T)
            if l == depth - 1 and nxt is not None:
                nx_src, nT = nxt
                nC = min(nT, 128)
                for i in range(nT // nC):
                    P.dma("sp", "xpre", xst[:nC, i, :], nx_src[i * nC:(i + 1) * nC, :], reads=Bxb, writes=[Bxst])
            ffn(l, 1, T)
            layer_norm(l, 2, T)
        for i in range(T // C):
            for half in range(2):
                ps, Bp = nbank()
                for cc in range(4):
                    c = half * 4 + cc
                    P.mm([I("transpose", ps[:C, cc * 128:(cc + 1) * 128], xres[:, c, i * C:(i + 1) * C], ident[:, :])], reads=[Bxr[c], Bc], writes=[Bp])
                P.op("act" if half == 0 else "dve", (I("activation", out=xio[:C, half * 512:(half + 1) * 512], in_=ps[:C, :], func=AF.Copy)) if half == 0 else (I("tensor_copy", out=xio[:C, half * 512:(half + 1) * 512], in_=ps[:C, :])), writes=[Bxio, Bp])
            P.dma("sp", "yout", y_dst[i * C:(i + 1) * C, :], xio[:C, :], reads=[Bxio])

    ntile = seqlen // TT
    tiles = []
    for sq_i in range(nseq):
        for t in range(ntile):
            r = sq_i * seqlen + t * TT
            last = (t == ntile - 1)
            tiles.append(dict(T=TT, x=xp[r:r + TT, :], y=yp[r:r + TT, :], first=(t == 0), s0=None,
                              s_out=[sp_out[l, sq_i] for l in range(depth)] if last else None, v_out=None))
    if has_sample:
        tiles.append(dict(T=64, x=xs, y=ys, first=True, s0=[st0[l] for l in range(depth)],
                          s_out=[ss_out[l] for l in range(depth)], v_out=[vs_out[l] for l in range(depth)]))
    for ti, tl in enumerate(tiles):
        direct[0] = (ti == 0)
        nxt = (tiles[ti + 1]["x"], tiles[ti + 1]["T"]) if ti + 1 < len(tiles) else None
        run_tile(tl["T"], tl["x"], tl["y"], tl["first"], tl["s0"], tl["s_out"], tl["v_out"], prefetched=(ti > 0), nxt=nxt)
    P.finish()
    P.run()
    print("sbuf bytes remaining", nc.sbuf_bytes_remaining)
    return nc, P


_CACHE = {}


def _prep_weights(inp, depth):
    f = lambda a: np.ascontiguousarray(np.asarray(a, dtype=np.float32))
    m = {
        "w1": f(inp["ffn_w1"][:depth]).reshape(depth * 2 * D, DFF),
        "w3": f(inp["ffn_w3"][:depth]).reshape(depth * 2 * D, DFF),
        "w2": f(inp["ffn_w2"][:depth]).reshape(depth * 2 * DFF, D),
        "win": f(inp["w_in"][:depth]).reshape(depth * D, INC),
        "wpa": f(inp["w_pa"][:depth]).reshape(depth * 512, D),
        "wpb": f(inp["w_pb"][:depth]).reshape(depth * D, D),
        "wo": f(inp["w_o"][:depth]).reshape(depth * D, D),
        "lng": f(inp["ln_g"][:depth]).reshape(depth * 24, 128),
        "lnb": f(inp["ln_b"][:depth]).reshape(depth * 24, 128),
        "gng": f(inp["gla_norm_g"][:depth]).reshape(depth * 2, 128),
        "gmg": f(inp["gm_ln_g"][:depth]), "gmb": f(inp["gm_ln_b"][:depth]),
        "bs": f(inp["gm_bs"][:depth]).reshape(depth, 512),
        "ba": f(inp["gla_ba"][:depth]),
        "ws": f(inp["gm_ws"][:depth]).reshape(depth * 4, 128, 128),
        "wa2": f(inp["gla_wa2"][:depth]),
    }
    return m


def run(inp, depth, nseq, seqlen, has_sample, ncores):
    key = (depth, nseq, seqlen, has_sample)
    if key not in _CACHE:
        _CACHE[key] = build(depth, nseq, seqlen, has_sample)
    nc, P = _CACHE[key]
    wm = _prep_weights(inp, depth)
    xp_all = np.asarray(inp["x_prompt"], dtype=np.float32)
    xs_all = np.asarray(inp["x_sample"], dtype=np.float32)
    st_all = np.asarray(inp["state_gla"], dtype=np.float32)
    in_maps = []
    for c in range(ncores):
        m = dict(wm)
        m["xp"] = np.ascontiguousarray(xp_all[c * nseq:(c + 1) * nseq, :seqlen]).reshape(nseq * seqlen, D)
        m["xs"] = np.ascontiguousarray(xs_all[c])
        m["st0"] = np.ascontiguousarray(st_all[:depth, c])
        in_maps.append(m)
    res = run_bass_kernel_spmd(nc, in_maps, core_ids=list(range(ncores)))
    R = res.results
    yp = np.concatenate([r["yp"].reshape(nseq, seqlen, D) for r in R], axis=0)
    ys = np.stack([r["ys"] for r in R], axis=0)
    spo = np.concatenate([r["sp_out"] for r in R], axis=1)
    sso = np.stack([r["ss_out"] for r in R], axis=1)
    vso = np.stack([r["vs_out"] for r in R], axis=1)
    return (yp.astype(np.float32), ys.astype(np.float32), spo.astype(np.float32), sso.astype(np.float32), vso.astype(np.float32))


def kernel(**inputs):
    return run(inputs, DEPTH, 2, 2048, True, 8)
```

```python
import numpy as np
import concourse.bass as bass
import concourse.mybir as mybir
from concourse.bass_utils import run_bass_kernel_spmd

F32 = mybir.dt.float32
BF16 = mybir.dt.bfloat16
AF = mybir.ActivationFunctionType
ALU = mybir.AluOpType

D = 1024
DFF = 2816
NF = DFF // 128
INC = 6160
O_ZU, O_ZV, O_Q, O_K, O_V, O_GG, O_GLR, O_GA, O_GB = 0, 512, 1024, 1536, 2048, 3072, 4096, 4112, 5136
DEPTH = 4
ALPHA = (2.0 * DEPTH) ** 0.25
EPS = 1e-5
TT = 512
N_FILL = 16


def I(name, *a, **kw):
    return (name, a, kw)


def _call(eng, d):
    return getattr(eng, d[0])(*d[1], **d[2])


class Buf:
    __slots__ = ("name", "w", "r")

    def __init__(self, name):
        self.name = name
        self.w = None
        self.r = {}


class Prog:
    ENGS = ("pe", "act", "dve", "pool", "sp")

    def __init__(self, nc):
        self.nc = nc
        self.sem = {}
        self.cnt = {}
        self.waited = {e: {} for e in self.ENGS}
        for e in self.ENGS:
            self.sem[e] = nc.alloc_semaphore(name="s_" + e)
            self.cnt[e] = 0
        self.q = {e: [] for e in self.ENGS}
        self.n_inst = 0

    def dma_sem(self, key):
        sk = ("dma", key)
        if sk not in self.sem:
            self.sem[sk] = self.nc.alloc_semaphore(name="d_" + str(key))
            self.cnt[sk] = 0
        return sk

    def _need(self, e, semkey, val):
        w = self.waited[e]
        if w.get(semkey, 0) >= val:
            return
        sem = self.sem[semkey]
        self.q[e].append(lambda eng, sem=sem, val=val: eng.wait_ge(sem, val))
        w[semkey] = val

    def _deps(self, e, reads, writes, raw_same=True):
        for b in reads:
            if b.w is not None:
                k, v = b.w
                if k == e and not raw_same:
                    continue
                self._need(e, k, v)
        for b in writes:
            if b.w is not None:
                k, v = b.w
                if k != e:
                    self._need(e, k, v)
            for k, v in b.r.items():
                if k != e:
                    self._need(e, k, v)

    def _mark(self, k, v, reads, writes):
        for b in reads:
            if b.r.get(k, 0) < v:
                b.r[k] = v
        for b in writes:
            b.w = (k, v)
            b.r = {}

    def op(self, e, fn, reads=(), writes=()):
        self._deps(e, reads, writes)
        sem = self.sem[e]
        self.q[e].append(lambda eng, fn=fn, sem=sem: _call(eng, fn).then_inc(sem, 1))
        self.cnt[e] += 1
        self._mark(e, self.cnt[e], reads, writes)
        self.n_inst += 1

    def mm(self, fns, reads=(), writes=(), per=None):
        e = "pe"
        self._deps(e, reads, writes, raw_same=False)
        sem = self.sem[e]
        n = len(fns)
        for i, fn in enumerate(fns):
            if per is not None:
                self._deps(e, per[i], (), raw_same=False)
            if i < n - 1:
                self.q[e].append(lambda eng, fn=fn: _call(eng, fn))
            else:
                self.q[e].append(lambda eng, fn=fn, sem=sem: _call(eng, fn).then_inc(sem, 1))
        self.cnt[e] += 1
        allreads = list(reads)
        if per is not None:
            for p_ in per:
                allreads.extend(p_)
        self._mark(e, self.cnt[e], allreads, writes)
        self.n_inst += n

    def dma(self, q, key, out, in_, reads=(), writes=()):
        sk = self.dma_sem(key)
        self._deps(q, reads, writes)
        sem = self.sem[sk]
        self.q[q].append(lambda eng, out=out, in_=in_, sem=sem: eng.dma_start(out=out, in_=in_).then_inc(sem, 16))
        self.cnt[sk] += 16
        self._mark(sk, self.cnt[sk], reads, writes)
        self.n_inst += 1

    def barrier(self, engs=("pe", "act", "dve", "pool")):
        for e in engs:
            for o in engs:
                if o != e and self.cnt[o] > 0:
                    self._need(e, o, self.cnt[o])

    def finish(self):
        for k, v in self.cnt.items():
            if v > 0 and k != "sp":
                self._need("sp", k, v)

    def run(self):
        nc = self.nc
        with nc.Block() as block:
            def mk(e):
                def f(eng):
                    for t in self.q[e]:
                        t(eng)
                return f
            block.tensor(mk("pe"))
            block.scalar(mk("act"))
            block.vector(mk("dve"))
            block.gpsimd(mk("pool"))
            block.sync(mk("sp"))


def build(depth, nseq, seqlen, has_sample, n_w_slots=5):
    nc = bass.Bass("TRN2", target_bir_lowering=False)
    P = Prog(nc)
    NTOK = nseq * seqlen

    def din(name, shape, dt=F32):
        return nc.dram_tensor(name, shape, dt, kind="ExternalInput").ap()

    def dout(name, shape):
        return nc.dram_tensor(name, shape, F32, kind="ExternalOutput").ap()

    xp = din("xp", [max(NTOK, 1), D])
    xs = din("xs", [64, D])
    st0 = din("st0", [depth, 4, 128, 256])
    w_f32 = {
        "w1": din("w1", [depth * 2 * D, DFF]), "w3": din("w3", [depth * 2 * D, DFF]),
        "w2": din("w2", [depth * 2 * DFF, D]), "win": din("win", [depth * D, INC]),
        "wpa": din("wpa", [depth * 512, D]), "wpb": din("wpb", [depth * D, D]), "wo": din("wo", [depth * D, D]),
    }
    lng_d = din("lng", [depth * 24, 128])
    lnb_d = din("lnb", [depth * 24, 128])
    gng_d = din("gng", [depth * 2, 128])
    gmg_d = din("gmg", [depth, 512])
    gmb_d = din("gmb", [depth, 512])
    bs_d = din("bs", [depth, 512])
    ba_d = din("ba", [depth, 512])
    ws_d = din("ws", [depth * 4, 128, 128])
    wa2_d = din("wa2", [depth, 16, 512])

    yp = dout("yp", [max(NTOK, 1), D])
    ys = dout("ys", [64, D])
    sp_out = dout("sp_out", [depth, max(nseq, 1), 4, 128, 256])
    ss_out = dout("ss_out", [depth, 4, 128, 256])
    vs_out = dout("vs_out", [depth, 64, 512])

    w_bf = {k: nc.dram_tensor(k + "_bf", list(v.shape), BF16).ap() for k, v in w_f32.items()}
    w_rows = {"w1": 2 * D, "w3": 2 * D, "w2": 2 * DFF, "win": D, "wpa": 512, "wpb": D, "wo": D}
    scrB = {(l, g): Buf("scr%d_%d" % (l, g)) for l in range(depth) for g in range(3)}
    GRP = {"w1": None, "w3": None, "w2": None, "win": 1, "wpa": 1, "wpb": 1, "wo": 1}

    def emit_conversions():
        for l in range(depth):
            for g in range(3):
                if g == 1:
                    items = [("win", l * D, D), ("wpa", l * 512, 512), ("wpb", l * D, D), ("wo", l * D, D)]
                else:
                    sidx = 0 if g == 0 else 1
                    items = [("w1", (l * 2 + sidx) * D, D), ("w3", (l * 2 + sidx) * D, D), ("w2", (l * 2 + sidx) * DFF, DFF)]
                for (k, r0, nr) in items:
                    step = 256
                    for a in range(0, nr, step):
                        b = min(nr, a + step)
                        P.dma("pool", "cvt%d_%d" % (l, g), w_bf[k][r0 + a:r0 + b, :], w_f32[k][r0 + a:r0 + b, :], writes=[scrB[(l, g)]])

    def sb(name, shape, dt=F32):
        return nc.alloc_sbuf_tensor(name, shape, dt)

    ident = sb("ident", [128, 128]); onesf = sb("onesf", [128, 128])
    ucum = sb("ucum", [128, 128]); lst = sb("lst", [128, 128])
    maskb = sb("maskb", [128, 128], BF16)
    ones_ln = sb("ones_ln", [128, 128], BF16); ones_rms = sb("ones_rms", [128, 128], BF16)
    tmpc = sb("tmpc", [128, 128])
    Bc = Buf("consts")
    P.op("pool", I("memset", onesf[:], 1.0), writes=[Bc])
    P.op("pool", I("memset", ones_ln[:], 1.0 / D), writes=[Bc])
    P.op("pool", I("memset", ones_rms[:], 1.0 / 256), writes=[Bc])
    P.op("pool", I("memset", tmpc[:], -1.0 / 16), writes=[Bc])
    dmy = sb("dmy", [128, 2]); Bdmy = Buf("dmy")
    eps_ln_t = sb("eps_ln_t", [128, 2])
    P.op("pool", I("memset", eps_ln_t[:, 0:1], EPS / (ALPHA * ALPHA)), writes=[Bc])
    P.op("pool", I("memset", eps_ln_t[:, 1:2], EPS), writes=[Bc])
    P.op("pool", I("memset", dmy[:], 1.0), writes=[Bc])

    def preload_ln_table():
        P.op("act", I("activation", out=dmy[:, 1:2], in_=dmy[:, 0:1], func=AF.Ln), reads=[Bc], writes=[Bdmy])
    P.op("pool", I("affine_select", out=ident[:], in_=onesf[:], pattern=[[-1, 128]], compare_op=ALU.is_equal, fill=0.0, base=0, channel_multiplier=1), reads=[Bc], writes=[Bc])
    P.op("pool", I("affine_select", out=ucum[:], in_=tmpc[:], pattern=[[1, 128]], compare_op=ALU.is_ge, fill=0.0, base=0, channel_multiplier=-1), reads=[Bc], writes=[Bc])
    P.op("pool", I("affine_select", out=lst[:], in_=tmpc[:], pattern=[[-1, 128]], compare_op=ALU.is_gt, fill=0.0, base=0, channel_multiplier=1), reads=[Bc], writes=[Bc])
    P.op("pool", I("affine_select", out=maskb[:], in_=onesf[:], pattern=[[1, 128]], compare_op=ALU.is_ge, fill=0.0, base=0, channel_multiplier=-1), reads=[Bc], writes=[Bc])

    banks = [(nc.alloc_psum_tensor("bank%d" % i, [128, 512], F32), Buf("bank%d" % i)) for i in range(8)]
    bank_i = [0]

    stats_active = [False]

    def nbank():
        b = banks[bank_i[0] % (6 if stats_active[0] else 8)]
        bank_i[0] += 1
        return b

    lngT = sb("lngT", [128, depth * 24]); lnbT = sb("lnbT", [128, depth * 24]); gngT = sb("gngT", [128, depth * 2])
    stage = sb("stage", [128, 128])
    Bst = Buf("stage")
    for (src, dst, n) in ((lng_d, lngT, depth * 24), (lnb_d, lnbT, depth * 24), (gng_d, gngT, depth * 2)):
        P.dma("sp", "misc", stage[0:n, :], src[:, :], writes=[Bst])
        ps, Bp = nbank()
        P.mm([I("transpose", ps[:, 0:n], stage[0:n, :], ident[0:n, 0:n])], reads=[Bst, Bc], writes=[Bp])
        P.op("dve", I("tensor_copy", out=dst[:, 0:n], in_=ps[:, 0:n]), writes=[Bp, Bc])
    WT = sb("WT", [128, depth * 4, 128], BF16)
    wstage = sb("wstage", [128, 128])
    for i in range(depth * 4):
        P.dma("sp", "misc", stage[:, :], ws_d[i], writes=[Bst])
        Bw = Buf("wst")
        P.op("pool", I("affine_select", out=wstage[:], in_=stage[:], pattern=[[-1, 128]], compare_op=ALU.is_ge, fill=0.0, base=0, channel_multiplier=1), reads=[Bst], writes=[Bw])
        ps, Bp = nbank()
        P.mm([I("transpose", ps[:, 0:128], wstage[:, :], ident[:, :])], reads=[Bw, Bc], writes=[Bp])
        P.op("dve", I("tensor_copy", out=WT[:, i, :], in_=ps[:, 0:128]), reads=[], writes=[Bp, Bc, Bst])
    wa2b = sb("wa2b", [16, depth, 512], BF16)
    for l in range(depth):
        P.dma("pool", "misc2", wa2b[:, l, :], wa2_d[l], writes=[Bc])

    S = [sb("S%d" % l, [128, 4, 256]) for l in range(depth)]
    BS = [Buf("S%d" % l) for l in range(depth)]
    Sbf = sb("Sbf", [128, 4, 256], BF16); BSbf = Buf("Sbf")
    xres = sb("xres", [128, 8, TT]); xbf = sb("xbf", [128, 8, TT], BF16)
    Bxr = [Buf("xr%d" % c) for c in range(8)]
    Bxb = [Buf("xb%d" % c) for c in range(8)]
    xio = sb("xio", [128, D]); Bxio = Buf("xio")
    vnf = sb("vnf", [128, 512]); Bvnf = Buf("vnf")
    lc = sb("lc", [128, 4, 512]); Blc = Buf("lc")

    direct = [True]
    blockB = {}
    pool_ok = [False]

    def PL(alt="dve"):
        return "pool" if pool_ok[0] else alt

    slots = [(sb("wslot%d" % i, [128, 4096], BF16), Buf("wslot%d" % i)) for i in range(n_w_slots)]
    slot_i = [0]

    def load_w(key, l, r0, nk, c0, ncols, g=1):
        def last_use(si):
            Bs = slots[si][1]
            if Bs.w is None:
                return -1
            return Bs.r.get("pe", 1 << 60)
        si = min(range(n_w_slots), key=lambda i_: (last_use(i_), (i_ - slot_i[0]) % n_w_slots))
        slot_i[0] = si + 1
        t, B = slots[si]
        v = t[:, 0:nk * ncols].rearrange("p (k n) -> p k n", k=nk)
        src = w_bf[key][r0:r0 + nk * 128, c0:c0 + ncols].rearrange("(k p) n -> p k n", p=128)
        blk = (key, r0, nk, c0, ncols)
        if direct[0]:
            src32 = w_f32[key][r0:r0 + nk * 128, c0:c0 + ncols].rearrange("(k p) n -> p k n", p=128)
            P.dma("pool", "ws%d" % si, v, src32, writes=[B])
            if blk not in blockB:
                blockB[blk] = Buf("blk")
            P.dma("sp", "wb%d" % si, src, v, reads=[B], writes=[blockB[blk]])
        else:
            P.dma("sp", "ws%d" % si, v, src, reads=[blockB[blk]], writes=[B])
        return v, B

    RN = 40700
    R = sb("R", [128, RN], BF16)

    class Carver:
        def __init__(self):
            self.off = 0

        def take(self, n_elems, dt, shape):
            nb = n_elems * (4 if dt == F32 else 2)
            nb = (nb + 31) // 32 * 32
            a = self.off
            self.off += nb // 2
            assert self.off <= RN, ("R overflow", self.off)
            v = R[:, a:a + nb // 2]
            if dt == F32:
                v = v.bitcast(F32)
            v = v[:, 0:n_elems]
            if len(shape) == 3:
                v = v.rearrange("p (a b) -> p a b", a=shape[1])
            return v

    CY_F = 0.5 / ALPHA
    CY_M = 1.0 / ALPHA
    EPS_LN = EPS / (ALPHA * ALPHA)

    ln_sqb = sb("ln_sqb", [128, 8, TT], BF16); Bln_sq = [Buf("lsq%d" % c) for c in range(8)]
    ln_mean = sb("ln_mean", [128, TT]); Bln_mean = Buf("lmean")
    ln_msq = sb("ln_msq", [128, TT]); Bln_msq = Buf("lmsq")
    ln_var = sb("ln_var", [128, TT]); Bln_var = Buf("lvar")
    ln_rstd = sb("ln_rstd", [128, TT]); Bln_rstd = Buf("lrstd")
    pm, Bpm = banks[6]
    pq, Bpq = banks[7]

    def ln_chunk_prep(c, T):
        P.op("act", I("activation", out=ln_sqb[:, c, :T], in_=xres[:, c, :T], func=AF.Square), reads=[Bxr[c]], writes=[Bln_sq[c]])
        P.op(PL(), I("tensor_copy", out=xbf[:, c, :T], in_=xres[:, c, :T]), reads=[Bxr[c]], writes=[Bxb[c]])

    def ln_chunk_stats(c, T):
        P.mm([I("matmul", pm[:, :T], lhsT=ones_ln[:, :], rhs=xbf[:, c, :T], start=(c == 0), stop=(c == 7))], reads=[Bxb[c], Bc], writes=[Bpm])
        P.mm([I("matmul", pq[:, :T], lhsT=ones_ln[:, :], rhs=ln_sqb[:, c, :T], start=(c == 0), stop=(c == 7))], reads=[Bln_sq[c], Bc], writes=[Bpq])

    def layer_norm(l, s, T):
        kw_ps, kw_B = nbank()

        def keep_warm(after):
            P.mm([I("matmul", kw_ps[:, 0:16], lhsT=ones_ln[:, :], rhs=ones_ln[:, 0:16], start=True, stop=True)], reads=[after, Bc], writes=[kw_B])

        P.mm([I("matmul", kw_ps[:, :T], lhsT=ones_ln[:, :], rhs=ln_sqb[:, 7, :T], start=True, stop=True) for _ in range(N_FILL)],
             reads=[Bln_sq[7], Bc], writes=[kw_B])
        P.op("act", I("activation", out=ln_msq[:, :T], in_=pm[:, :T], func=AF.Square), writes=[Bln_msq, Bpm])
        P.op("act", I("activation", out=ln_mean[:, :T], in_=pm[:, :T], func=AF.Copy), writes=[Bln_mean, Bpm])
        keep_warm(Bln_mean)
        P.op("dve", I("tensor_tensor", out=ln_var[:, :T], in0=pq[:, :T], in1=ln_msq[:, :T], op=ALU.subtract), reads=[Bln_msq], writes=[Bln_var, Bpq])
        stats_active[0] = False
        P.op("act", I("activation", out=ln_var[:, :T], in_=ln_var[:, :T], func=AF.Ln, bias=eps_ln_t[:, 0:1], scale=1.0), reads=[Bln_var, Bc], writes=[Bln_var])
        keep_warm(Bln_var)
        P.op("act", I("activation", out=ln_rstd[:, :T], in_=ln_var[:, :T], func=AF.Exp, scale=-0.5), reads=[Bln_var], writes=[Bln_rstd])
        keep_warm(Bln_rstd)
        col = (l * 3 + s) * 8
        gcs = [lngT[:, col + c:col + c + 1] for c in range(8)]
        bcs = [lnbT[:, col + c:col + c + 1] for c in range(8)]

        def sub(c):
            P.op("dve", I("tensor_tensor", out=xres[:, c, :T], in0=xres[:, c, :T], in1=ln_mean[:, :T], op=ALU.subtract), reads=[Bxr[c], Bln_mean], writes=[Bxr[c]])

        def mul(c):
            P.op("dve", I("tensor_tensor", out=xres[:, c, :T], in0=xres[:, c, :T], in1=ln_rstd[:, :T], op=ALU.mult), reads=[Bxr[c], Bln_rstd], writes=[Bxr[c]])
            P.op("act", I("activation", out=xbf[:, c, :T], in_=xres[:, c, :T], func=AF.Identity, scale=gcs[c], bias=bcs[c]), reads=[Bxr[c], Bc], writes=[Bxb[c]])

        for c in range(4):
            sub(c)
        for c in range(4):
            mul(c)
            sub(c + 4)
        for c in range(4, 8):
            mul(c)
        for c in range(8):
            P.op("dve", I("tensor_scalar", out=xres[:, c, :T], in0=xres[:, c, :T], scalar1=gcs[c], scalar2=bcs[c], op0=ALU.mult, op1=ALU.add), reads=[Bxr[c], Bc], writes=[Bxr[c]])

    Bh = [Buf("h%d" % j) for j in range(NF)]
    Bsil = [Buf("sil0"), Buf("sil1"), Buf("sil2")]

    def ffn(l, s, T):
        cv = Carver()
        hT = cv.take(NF * TT, BF16, [128, NF, TT])
        sil = [cv.take(TT, F32, [128, TT]) for _ in range(3)]
        r0 = (l * 2 + s) * D
        for blk in range(0, NF, 4):
            nch = min(4, NF - blk)
            w1v, B1 = load_w("w1", l, r0, 8, blk * 128, nch * 128, g=2 * s)
            w3v, B3 = load_w("w3", l, r0, 8, blk * 128, nch * 128, g=2 * s)
            for jj in range(nch):
                j = blk + jj
                pa, Bpa = nbank()
                P.mm([I("matmul", pa[:, :T], lhsT=w1v[:, k, jj * 128:(jj + 1) * 128], rhs=xbf[:, k, :T], start=(k == 0), stop=(k == 7)) for k in range(8)], reads=[B1], per=[[Bxb[k]] for k in range(8)], writes=[Bpa])
                pb, Bpb = nbank()
                P.mm([I("matmul", pb[:, :T], lhsT=w3v[:, k, jj * 128:(jj + 1) * 128], rhs=xbf[:, k, :T], start=(k == 0), stop=(k == 7)) for k in range(8)], reads=[B3], per=[[Bxb[k]] for k in range(8)], writes=[Bpb])
                st, Bs_ = sil[j % 3], Bsil[j % 3]
                P.op("act", I("activation", out=st[:, :T], in_=pa[:, :T], func=AF.Silu), writes=[Bs_, Bpa])
                P.op("dve", I("tensor_tensor", out=hT[:, j, :T], in0=pb[:, :T], in1=st[:, :T], op=ALU.mult), reads=[Bs_], writes=[Bh[j], Bpb])
        r2 = (l * 2 + s) * DFF
        preload_ln_table()
        stats_active[0] = True
        for c in range(8):
            halves = []
            for (k0, nk) in ((0, 16), (16, 6)):
                halves.append(load_w("w2", l, r2 + k0 * 128, nk, c * 128, 128, g=2 * s))
            py, Bpy = nbank()
            fns = []
            per = []
            for hi, (k0, nk) in enumerate(((0, 16), (16, 6))):
                wv = halves[hi][0]
                for k in range(nk):
                    fns.append(I("matmul", py[:, :T], lhsT=wv[:, k, :], rhs=hT[:, k0 + k, :T], start=(k0 + k == 0), stop=(k0 + k == NF - 1)))
                    per.append([Bh[k0 + k], halves[hi][1]])
            P.mm(fns, reads=[], writes=[Bpy], per=per)
            P.op("dve", I("scalar_tensor_tensor", out=xres[:, c, :T], in0=py[:, :T], scalar=CY_F, in1=xres[:, c, :T], op0=ALU.mult, op1=ALU.add), reads=[Bxr[c]], writes=[Bxr[c], Bpy])
            ln_chunk_prep(c, T)
            if c > 0:
                ln_chunk_stats(c - 1, T)
        ln_chunk_stats(7, T)

    def mixer(l, T, first, s0_ap, s_out_ap, v_out_ap):
        C = min(T, 128)
        NCH = T // C
        cv = Carver()
        uT = cv.take(4 * TT, BF16, [128, 4, TT]); Bu = [Buf("u%d" % g) for g in range(4)]
        gv = [cv.take(512, F32, [128, 512]) for _ in range(2)] + [xio[:, 0:512], xio[:, 512:1024]]
        Bgv = [Buf("gv0"), Buf("gv1"), Bxio, Bxio]
        vnb = cv.take(4 * 512, BF16, [128, 4, 512]); Bvnb = [Buf("vnb%d" % i) for i in range(4)]
        stt = cv.take(32, F32, [128, 4, 8]); Bstt = [Buf("stt%d" % i) for i in range(4)]
        mv = cv.take(16, F32, [128, 4, 4]); Bmv = Buf("mv")
        qk = cv.take(8 * TT, BF16, [128, 8, TT]); Bq = Buf("qT"); Bk = Buf("kT")
        qT = qk[:, 0:4, :]; kT = qk[:, 4:8, :]; mT = qk
        sgg = cv.take(8 * TT, BF16, [128, 8, TT]); Bsgg = [Buf("sgg%d" % i) for i in range(4)]
        glrT = cv.take(TT, BF16, [128, TT]); Bglr = Buf("glr")
        vtok = cv.take(4 * 1024, BF16, [128, 4, 1024]); Bvt = [Buf("vt%d" % i) for i in range(4)]
        kd = cv.take(4 * 512, BF16, [128, 4, 512]); Bkd = [Buf("kd%d" % i) for i in range(4)]
        lgp = [cv.take(512, F32, [128, 512]) for _ in range(2)]; Blg = [Buf("lgp0"), Buf("lgp1")]
        ekd = [cv.take(512, F32, [128, 512]) for _ in range(2)]; Bekd = [Buf("ekd0"), Buf("ekd1")]
        eb = [cv.take(512, F32, [128, 4, 128]) for _ in range(2)]; Beb = [Buf("eb0"), Buf("eb1")]
        enb = [cv.take(512, F32, [128, 4, 128]) for _ in range(2)]; Benb = [Buf("enb0"), Buf("enb1")]
        ebl = [cv.take(16, F32, [128, 4, 4]) for _ in range(2)]; Bebl = [Buf("ebl0"), Buf("ebl1")]
        qe = [cv.take(512, BF16, [128, 4, 128]) for _ in range(2)]; Bqe = [Buf("qe0"), Buf("qe1")]
        ke = [cv.take(512, BF16, [128, 4, 128]) for _ in range(2)]; Bke = [Buf("ke0"), Buf("ke1")]
        attm = [cv.take(512, BF16, [128, 4, 128]) for _ in range(2)]; Batt = [Buf("att0"), Buf("att1")]
        sq = [cv.take(1024, BF16, [128, 8, 128]) for _ in range(2)]; Bsq = [Buf("osq0"), Buf("osq1")]
        rsd = [cv.take(512, F32, [128, 4, 128]) for _ in range(2)]; Brsd = [Buf("rsd0"), Buf("rsd1")]
        spt = rsd[0]; Bspt = Brsd[0]
        otmp = [cv.take(1024, F32, [128, 8, 128]) for _ in range(2)]; Bot = [Buf("ot0"), Buf("ot1")]
        sga = vnf; Bsga = Bvnf
        m1s = gv; Bm1s = Bgv
        Bm = [Buf("m%d" % c) for c in range(8)]
        assert cv.off <= RN, cv.off
        r0 = l * D

        for i, src in enumerate((gmg_d, gmb_d, bs_d, ba_d)):
            P.dma("sp", "lc", lc[:, i, :], src[l:l + 1, :].partition_broadcast(128), writes=[Blc])

        def proj_fm(c0, nchunks, evac):
            for b0 in range(0, nchunks, 4):
                nb = min(4, nchunks - b0)
                wv, Bw = load_w("win", l, r0, 8, c0 + b0 * 128, nb * 128)
                for jj in range(nb):
                    ps, Bp = nbank()
                    P.mm([I("matmul", ps[:, :T], lhsT=wv[:, k, jj * 128:(jj + 1) * 128], rhs=xbf[:, k, :T], start=(k == 0), stop=(k == 7)) for k in range(8)], reads=[Bw], per=[[Bxb[k]] for k in range(8)], writes=[Bp])
                    evac(b0 + jj, ps, Bp)

        proj_fm(O_ZU, 4, lambda j, ps, Bp: P.op("act", I("activation", out=uT[:, j, :T], in_=ps[:, :T], func=AF.Gelu), writes=[Bu[j], Bp]))
        wzv, Bwzv = load_w("win", l, r0, 8, O_ZV, 512)
        def zv1(i):
            ps, Bp = nbank()
            P.mm([I("matmul", ps[:C, :], lhsT=xbf[:, k, i * C:(i + 1) * C], rhs=wzv[:, k, :], start=(k == 0), stop=(k == 7)) for k in range(8)], reads=[Bwzv], writes=[Bp], per=[[Bxb[k]] for k in range(8)])
            g_ = gv[i]; Bg_ = Bgv[i]; st_ = stt[:, i, :]; mv_ = mv[:, i, :]
            P.op("act", I("activation", out=g_[:C, :], in_=ps[:C, :], func=AF.Gelu), writes=[Bg_, Bp])
            P.op("dve", I("bn_stats", out=st_[:C, 0:6], in_=g_[:C, :]), reads=[Bg_], writes=[Bstt[i]])
            P.op("dve", I("bn_aggr", out=mv_[:C, 0:2], in_=st_[:C, 0:6]), reads=[Bstt[i]], writes=[Bmv])
            P.op("dve", I("tensor_scalar", out=mv_[:C, 2:3], in0=mv_[:C, 1:2], scalar1=EPS, scalar2=None, op0=ALU.add), reads=[Bmv], writes=[Bmv])

        def zv2(i):
            g_ = gv[i]; Bg_ = Bgv[i]; mv_ = mv[:, i, :]
            P.op("dve", I("tensor_scalar", out=g_[:C, :], in0=g_[:C, :], scalar1=mv_[:C, 0:1], scalar2=mv_[:C, 3:4], op0=ALU.subtract, op1=ALU.mult), reads=[Bg_, Bmv], writes=[Bg_])
            P.op("dve", I("tensor_tensor", out=g_[:C, :], in0=g_[:C, :], in1=lc[:C, 0, :], op=ALU.mult), reads=[Bg_, Blc], writes=[Bg_])
            if v_out_ap is not None:
                P.op("dve", I("tensor_tensor", out=vnf[:C, :], in0=g_[:C, :], in1=lc[:C, 1, :], op=ALU.add), reads=[Bg_, Blc], writes=[Bvnf])
                P.dma("sp", "vout", v_out_ap, vnf[:C, :], reads=[Bvnf])
            P.op("dve", I("tensor_tensor", out=vnb[:C, i, :], in0=g_[:C, :], in1=lc[:C, 1, :], op=ALU.add), reads=[Bg_, Blc], writes=[Bvnb[i]])

        for i in range(NCH):
            zv1(i)
        proj_fm(O_Q, 4, lambda j, ps, Bp: P.op("act", I("activation", out=qT[:, j, :T], in_=ps[:, :T], func=AF.Copy, scale=128.0 ** -0.5), writes=[Bq, Bp]))
        P.op("act", I("activation", out=mv[:C, 0:NCH, 2:3], in_=mv[:C, 0:NCH, 2:3], func=AF.Sqrt), reads=[Bmv], writes=[Bmv])
        P.op("dve", I("reciprocal", out=mv[:C, 0:NCH, 3:4], in_=mv[:C, 0:NCH, 2:3]), reads=[Bmv], writes=[Bmv])
        proj_fm(O_K, 4, lambda j, ps, Bp: P.op("act", I("activation", out=kT[:, j, :T], in_=ps[:, :T], func=AF.Copy), writes=[Bk, Bp]))
        for i in range(NCH):
            zv2(i)
        wgl, Bwgl = load_w("win", l, r0, 8, O_GLR, 16)
        ps, Bp = nbank()
        P.mm([I("matmul", ps[:16, :T], lhsT=wgl[:, k, :], rhs=xbf[:, k, :T], start=(k == 0), stop=(k == 7)) for k in range(8)], reads=[Bwgl], per=[[Bxb[k]] for k in range(8)], writes=[Bp])
        P.op("act", I("activation", out=glrT[:16, :T], in_=ps[:16, :T], func=AF.Copy), writes=[Bglr, Bp])

        fillers = []
        gg_state = {}

        def gg_task(j):
            def run():
                blk = j // 4
                if blk not in gg_state:
                    gg_state[blk] = load_w("win", l, r0, 8, O_GG + blk * 512, 512)
                wv, Bw = gg_state[blk]
                jj = j % 4
                ps, Bp = nbank()
                P.mm([I("matmul", ps[:, :T], lhsT=wv[:, k, jj * 128:(jj + 1) * 128], rhs=xbf[:, k, :T], start=(k == 0), stop=(k == 7)) for k in range(8)], reads=[Bw], per=[[Bxb[k]] for k in range(8)], writes=[Bp])
                P.op("act", I("activation", out=sgg[:, j, :T], in_=ps[:, :T], func=AF.Silu), writes=Bsgg + [Bp])
            return run

        def spatial_task(i):
            def run():
                ps, Bp = nbank()
                for g in range(4):
                    P.mm([I("matmul", ps[:, g * C:(g + 1) * C], lhsT=vnb[:C, i, g * 128:(g + 1) * 128], rhs=WT[:C, l * 4 + g, :C], start=True, stop=True)], reads=[Bvnb[i], Bc], writes=[Bp])
                pv = ps[:, 0:4 * C].rearrange("p (g c) -> p g c", g=4)
                P.op("dve", I("tensor_tensor", out=spt[:, :, :C], in0=pv, in1=lc[:, 2, :].rearrange("p (g c) -> p g c", g=4)[:, :, :C], op=ALU.add), reads=[Blc], writes=[Bspt, Bp])
                P.op("dve", I("tensor_tensor", out=uT[:, :, i * C:(i + 1) * C], in0=uT[:, :, i * C:(i + 1) * C], in1=spt[:, :, :C], op=ALU.mult), reads=[Bspt] + Bu, writes=Bu)
            return run

        mg_state = {}

        def merge_p1(hb, cc):
            def run():
                if hb not in mg_state:
                    mg_state[hb] = (load_w("wpa", l, l * 512, 4, hb * 512, 512), load_w("win", l, r0, 8, O_GA + hb * 512, 512))
                (wpa_v, Bwpa), (wga_v, Bwga) = mg_state[hb]
                cs = slice(cc * 128, (cc + 1) * 128)
                pg, Bpg = nbank()
                P.mm([I("matmul", pg[:, :T], lhsT=wga_v[:, k, cs], rhs=xbf[:, k, :T], start=(k == 0), stop=(k == 7)) for k in range(8)], reads=[Bwga], per=[[Bxb[k]] for k in range(8)], writes=[Bpg])
                pp, Bpp = nbank()
                P.mm([I("matmul", pp[:, :T], lhsT=wpa_v[:, k, cs], rhs=uT[:, k, :T], start=(k == 0), stop=(k == 3)) for k in range(4)], reads=[Bwpa], writes=[Bpp], per=[[Bu[k]] for k in range(4)])
                P.op("act", I("activation", out=sga[:, :T], in_=pg[:, :T], func=AF.Sigmoid), writes=[Bsga, Bpg])
                P.op("dve", I("tensor_tensor", out=m1s[cc][:, :T], in0=pp[:, :T], in1=sga[:, :T], op=ALU.mult), reads=[Bsga], writes=[Bm1s[cc], Bpp])
            return run

        fill_plan = {}
        fill_plan.setdefault(0, []).extend(gg_task(j) for j in range(4))
        fill_plan.setdefault(1, []).extend(gg_task(j) for j in range(4, 8))
        fill_plan.setdefault(2, []).extend(spatial_task(i) for i in range(NCH))
        fill_plan.setdefault(NCH + 1, []).extend(merge_p1(0, cc) for cc in range(2))
        fill_plan.setdefault(NCH + 2, []).extend(merge_p1(0, cc) for cc in range(2, 4))
        pending_tasks = []
        if first:
            if s0_ap is None:
                P.op(PL(), I("memset", S[l][:], 0.0), writes=[BS[l]])
            else:
                P.dma("sp", "s0", S[l][:], s0_ap.rearrange("h k v -> k h v"), writes=[BS[l]])
        P.op("act", I("activation", out=Sbf[:], in_=S[l][:], func=AF.Copy), reads=[BS[l]], writes=[BSbf])
        wk, Bwk = load_w("win", l, r0, 8, O_K, 512)
        wv0, Bwv0 = load_w("win", l, r0, 8, O_V, 512)
        wv1, Bwv1 = load_w("win", l, r0, 8, O_V + 512, 512)
        held = {}

        def stA(i):
            b2 = i % 2
            tok = slice(i * C, (i + 1) * C)
            for hv, (wv_, Bwv_) in enumerate(((wv0, Bwv0), (wv1, Bwv1))):
                ps, Bp = nbank()
                P.mm([I("matmul", ps[:C, :], lhsT=xbf[:, k, tok], rhs=wv_[:, k, :], start=(k == 0), stop=(k == 7)) for k in range(8)], reads=[Bwv_], per=[[Bxb[k]] for k in range(8)], writes=[Bp])
                if hv == 0:
                    P.op("act", I("activation", out=vtok[:C, i, 0:512], in_=ps[:C, :], func=AF.Copy), writes=[Bvt[i], Bp])
                else:
                    P.op("dve", I("tensor_copy", out=vtok[:C, i, 512:1024], in_=ps[:C, :]), writes=[Bvt[i], Bp])
            ps, Bp = nbank()
            P.mm([I("matmul", ps[:C, :], lhsT=glrT[:16, tok], rhs=wa2b[:, l, :], start=True, stop=True)], reads=[Bglr, Bc], writes=[Bp])
            P.op("dve", I("tensor_tensor", out=lgp[b2][:C, :], in0=ps[:C, :], in1=lc[:C, 3, :], op=ALU.add), reads=[Blc], writes=[Blg[b2], Bp])
            P.op("act", I("activation", out=lgp[b2][:C, :], in_=lgp[b2][:C, :], func=AF.Exp, scale=-1.0), reads=[Blg[b2]], writes=[Blg[b2]])
            P.op("act", I("activation", out=lgp[b2][:C, :], in_=lgp[b2][:C, :], func=AF.Ln, bias=1.0, scale=1.0), reads=[Blg[b2]], writes=[Blg[b2]])

        def stB(i):
            b2 = i % 2
            tok = slice(i * C, (i + 1) * C)
            ps, Bp = nbank()
            P.mm([I("matmul", ps[:C, :], lhsT=lst[:C, :C], rhs=lgp[b2][:C, :], start=True, stop=True)], reads=[Blg[b2], Bc], writes=[Bp])
            P.op("act", I("activation", out=ekd[b2][:C, :], in_=ps[:C, :], func=AF.Exp), writes=[Bekd[b2], Bp])
            ps2, Bp2 = nbank()
            for h in range(4):
                P.mm([I("matmul", ps2[:, h * C:(h + 1) * C], lhsT=lgp[b2][:C, h * 128:(h + 1) * 128], rhs=ucum[:C, :C], start=True, stop=True)], reads=[Blg[b2], Bc], writes=[Bp2])
            pv = ps2[:, 0:4 * C].rearrange("p (h c) -> p h c", h=4)
            P.op("act", I("activation", out=eb[b2][:, :, :C], in_=pv, func=AF.Exp), writes=[Beb[b2], Bp2])
            P.op("act", I("activation", out=enb[b2][:, :, :C], in_=pv, func=AF.Exp, scale=-1.0), writes=[Benb[b2], Bp2])
            P.op("act", I("activation", out=ebl[b2][:, :, 0:1], in_=pv[:, :, C - 1:C], func=AF.Exp), writes=[Bebl[b2], Bp2])
            ps3, Bp3 = nbank()
            P.mm([I("matmul", ps3[:C, :], lhsT=xbf[:, k, tok], rhs=wk[:, k, :], start=(k == 0), stop=(k == 7)) for k in range(8)], reads=[Bwk], per=[[Bxb[k]] for k in range(8)], writes=[Bp3])
            P.op("dve", I("tensor_tensor", out=kd[:C, i, :], in0=ps3[:C, :], in1=ekd[b2][:C, :], op=ALU.mult), reads=[Bekd[b2]], writes=[Bkd[i], Bp3])
            P.op("dve", I("tensor_tensor", out=qe[b2][:, :, :C], in0=qT[:, :, tok], in1=eb[b2][:, :, :C], op=ALU.mult), reads=[Bq, Beb[b2]], writes=[Bqe[b2]])
            P.op(PL(), I("tensor_tensor", out=ke[b2][:, :, :C], in0=kT[:, :, tok], in1=enb[b2][:, :, :C], op=ALU.mult), reads=[Bk, Benb[b2]], writes=[Bke[b2]])

        def stC(i):
            b2 = i % 2
            ps, Bp = nbank()
            for h in range(4):
                P.mm([I("matmul", ps[:C, h * C:(h + 1) * C], lhsT=ke[b2][:, h, :C], rhs=qe[b2][:, h, :C], start=True, stop=True)], reads=[Bke[b2], Bqe[b2]], writes=[Bp])
            pv = ps[:C, 0:4 * C].rearrange("p (h c) -> p h c", h=4)
            P.op("dve", I("tensor_tensor", out=attm[b2][:C, :, :C], in0=pv, in1=maskb[:C, :C].unsqueeze(1).to_broadcast([C, 4, C]), op=ALU.mult), reads=[Bc], writes=[Batt[b2], Bp])

        def stD(i):
            b2 = i % 2
            po = [nbank(), nbank()]
            for j in range(8):
                h = j // 2
                pj, Bpj = po[j // 4]
                jc = (j % 4) * C
                P.mm([I("matmul", pj[:, jc:jc + C], lhsT=vtok[:C, i, j * 128:(j + 1) * 128], rhs=attm[b2][:C, h, :C], start=True, stop=False),
                      I("matmul", pj[:, jc:jc + C], lhsT=Sbf[:, h, (j % 2) * 128:(j % 2 + 1) * 128], rhs=qe[b2][:, h, :C], start=False, stop=True)],
                     reads=[Bvt[i], Batt[b2], BSbf, Bqe[b2]], writes=[Bpj])
            for hh in range(2):
                pj, Bpj = po[hh]
                pv = pj[:, 0:4 * C].rearrange("p (j c) -> p j c", j=4)
                P.op("act", I("activation", out=sq[b2][:, hh * 4:(hh + 1) * 4, :C], in_=pv, func=AF.Square), writes=[Bsq[b2], Bpj])
                P.op("act", I("activation", out=otmp[b2][:, hh * 4:(hh + 1) * 4, :C], in_=pv, func=AF.Copy), writes=[Bot[b2], Bpj])
            pS = [nbank(), nbank()]
            for h in range(4):
                pj, Bpj = pS[h // 2]
                P.mm([I("matmul", pj[:, (h % 2) * 256:(h % 2 + 1) * 256], lhsT=kd[:C, i, h * 128:(h + 1) * 128], rhs=vtok[:C, i, h * 256:(h + 1) * 256], start=True, stop=True)], reads=[Bkd[i], Bvt[i]], writes=[Bpj])
            for h in range(4):
                pj, Bpj = pS[h // 2]
                P.op("dve", I("scalar_tensor_tensor", out=S[l][:, h, :], in0=S[l][:, h, :], scalar=ebl[b2][:, h, 0:1], in1=pj[:, (h % 2) * 256:(h % 2 + 1) * 256], op0=ALU.mult, op1=ALU.add), reads=[Bebl[b2], BS[l]], writes=[BS[l], Bpj])
            P.op("act", I("activation", out=Sbf[:], in_=S[l][:], func=AF.Copy), reads=[BS[l]], writes=[BSbf])

        def stE(i):
            b2 = i % 2
            tok = slice(i * C, (i + 1) * C)
            pr, Bpr = nbank()
            for h in range(4):
                P.mm([I("matmul", pr[:, h * C:(h + 1) * C], lhsT=ones_rms[:, :], rhs=sq[b2][:, 2 * h, :C], start=True, stop=False),
                      I("matmul", pr[:, h * C:(h + 1) * C], lhsT=ones_rms[:, :], rhs=sq[b2][:, 2 * h + 1, :C], start=False, stop=True)], reads=[Bsq[b2], Bc], writes=[Bpr])
            pv = pr[:, 0:4 * C].rearrange("p (h c) -> p h c", h=4)
            P.op("act", I("activation", out=rsd[b2][:, :, :C], in_=pv, func=AF.Ln, bias=eps_ln_t[:, 1:2], scale=1.0), reads=[Bc], writes=[Brsd[b2], Bpr])
            P.op("act", I("activation", out=rsd[b2][:, :, :C], in_=rsd[b2][:, :, :C], func=AF.Exp, scale=-0.5), reads=[Brsd[b2]], writes=[Brsd[b2]])
            for par in range(2):
                ov = otmp[b2][:, :, :C].rearrange("p (h two) c -> p h two c", two=2)[:, :, par, :]
                P.op("dve", I("scalar_tensor_tensor", out=ov, in0=ov, scalar=gngT[:, l * 2 + par:l * 2 + par + 1], in1=rsd[b2][:, :, :C], op0=ALU.mult, op1=ALU.mult), reads=[Brsd[b2], Bot[b2], Bc], writes=[Bot[b2]])
            P.op(PL(), I("tensor_tensor", out=sgg[:, :, tok], in0=sgg[:, :, tok], in1=otmp[b2][:, :, :C], op=ALU.mult), reads=[Bot[b2], Bsgg[i]], writes=[Bsgg[i]])

        stages = [stA, stB, stC, stD, stE]
        for step in range(NCH + len(stages) - 1):
            for si in reversed(range(len(stages))):
                i = step - si
                if 0 <= i < NCH:
                    stages[si](i)
            for task in fill_plan.pop(step, ()):
                task()
        for k_ in sorted(fill_plan):
            pending_tasks.extend(fill_plan[k_])
        if s_out_ap is not None:
            P.dma("sp", "sout%d" % l, s_out_ap.rearrange("h k v -> k h v"), S[l][:], reads=[BS[l]])
        for t_ in pending_tasks:
            t_()
        del pending_tasks[:]
        for hb in range(2):
            if hb == 1:
                for cc in range(4):
                    merge_p1(1, cc)()
            wpb_v, Bwpb = load_w("wpb", l, l * D, 8, hb * 512, 512)
            wgb_v, Bwgb = load_w("win", l, r0, 8, O_GB + hb * 512, 512)
            for cc in range(4):
                c = hb * 4 + cc
                cs = slice(cc * 128, (cc + 1) * 128)
                pg2, Bpg2 = nbank()
                P.mm([I("matmul", pg2[:, :T], lhsT=wgb_v[:, k, cs], rhs=xbf[:, k, :T], start=(k == 0), stop=(k == 7)) for k in range(8)], reads=[Bwgb], per=[[Bxb[k]] for k in range(8)], writes=[Bpg2])
                pp2, Bpp2 = nbank()
                P.mm([I("matmul", pp2[:, :T], lhsT=wpb_v[:, k, cs], rhs=sgg[:, k, :T], start=(k == 0), stop=(k == 7)) for k in range(8)], reads=Bsgg + [Bwpb], writes=[Bpp2])
                P.op("act", I("activation", out=sga[:, :T], in_=pg2[:, :T], func=AF.Sigmoid), writes=[Bsga, Bpg2])
                P.op("dve", I("tensor_tensor", out=sga[:, :T], in0=pp2[:, :T], in1=sga[:, :T], op=ALU.mult), reads=[Bsga], writes=[Bsga, Bpp2])
                P.op("dve", I("tensor_tensor", out=mT[:, c, :T], in0=m1s[cc][:, :T], in1=sga[:, :T], op=ALU.add), reads=[Bm1s[cc], Bsga], writes=[Bm[c], Bq, Bk])
        preload_ln_table()
        stats_active[0] = True
        for hb in range(2):
            wo_v, Bwo = load_w("wo", l, l * D, 8, hb * 512, 512)
            for cc in range(4):
                c = hb * 4 + cc
                cs = slice(cc * 128, (cc + 1) * 128)
                py, Bpy = nbank()
                P.mm([I("matmul", py[:, :T], lhsT=wo_v[:, k, cs], rhs=mT[:, k, :T], start=(k == 0), stop=(k == 7)) for k in range(8)], reads=[Bwo], writes=[Bpy], per=[[Bm[k]] for k in range(8)])
                P.op("dve", I("scalar_tensor_tensor", out=xres[:, c, :T], in0=py[:, :T], scalar=CY_M, in1=xres[:, c, :T], op0=ALU.mult, op1=ALU.add), reads=[Bxr[c]], writes=[Bxr[c], Bpy])
                ln_chunk_prep(c, T)
                if c > 0:
                    ln_chunk_stats(c - 1, T)
        ln_chunk_stats(7, T)

    XST0 = 16000
    xst = R[:, XST0:XST0 + 8192].bitcast(F32).rearrange("p (a b) -> p a b", a=4)
    Bxst = Buf("xst")

    def run_tile(T, x_src, y_dst, first, s0, s_out, v_out, prefetched=False, nxt=None):
        C = min(T, 128)
        for i in range(T // C):
            if prefetched:
                src_t, Bsrc = xst[:, i, :], Bxst
            else:
                P.dma("sp", "xin", xio[:C, :], x_src[i * C:(i + 1) * C, :], writes=[Bxio])
                src_t, Bsrc = xio, Bxio
            for half in range(2):
                ps, Bp = nbank()
                for cc in range(4):
                    c = half * 4 + cc
                    P.mm([I("transpose", ps[:, cc * C:(cc + 1) * C], src_t[:C, c * 128:(c + 1) * 128], ident[:C, :C])], reads=[Bsrc, Bc], writes=[Bp])
                pv = ps[:, 0:4 * C].rearrange("p (a b) -> p a b", a=4)
                P.op("act", I("activation", out=xres[:, half * 4:(half + 1) * 4, i * C:(i + 1) * C], in_=pv, func=AF.Copy), writes=Bxr[half * 4:(half + 1) * 4] + [Bp])
                P.op("dve", I("tensor_copy", out=xbf[:, half * 4:(half + 1) * 4, i * C:(i + 1) * C], in_=pv), writes=Bxb[half * 4:(half + 1) * 4] + [Bp])
        for l in range(depth):
            ffn(l, 0, T)
            layer_norm(l, 0, T)
            mixer(l, T, first, None if s0 is None else s0[l], None if s_out is None else s_out[l], None if v_out is None else v_out[l])
            layer_norm(l, 1, T)
            if l == depth - 1 and nxt is not None:
                nx_src, nT = nxt
                nC = min(nT, 128)
                for i in range(nT // nC):
                    P.dma("sp", "xpre", xst[:nC, i, :], nx_src[i * nC:(i + 1) * nC, :], reads=Bxb, writes=[Bxst])
            ffn(l, 1, T)
            layer_norm(l, 2, T)
        for i in range(T // C):
            for half in range(2):
                ps, Bp = nbank()
                for cc in range(4):
                    c = half * 4 + cc
                    P.mm([I("transpose", ps[:C, cc * 128:(cc + 1) * 128], xres[:, c, i * C:(i + 1) * C], ident[:, :])], reads=[Bxr[c], Bc], writes=[Bp])
                P.op("act" if half == 0 else "dve", (I("activation", out=xio[:C, half * 512:(half + 1) * 512], in_=ps[:C, :], func=AF.Copy)) if half == 0 else (I("tensor_copy", out=xio[:C, half * 512:(half + 1) * 512], in_=ps[:C, :])), writes=[Bxio, Bp])
            P.dma("sp", "yout", y_dst[i * C:(i + 1) * C, :], xio[:C, :], reads=[Bxio])

    ntile = seqlen // TT
    tiles = []
    for sq_i in range(nseq):
        for t in range(ntile):
            r = sq_i * seqlen + t * TT
            last = (t == ntile - 1)
            tiles.append(dict(T=TT, x=xp[r:r + TT, :], y=yp[r:r + TT, :], first=(t == 0), s0=None,
                              s_out=[sp_out[l, sq_i] for l in range(depth)] if last else None, v_out=None))
    if has_sample:
        tiles.append(dict(T=64, x=xs, y=ys, first=True, s0=[st0[l] for l in range(depth)],
                          s_out=[ss_out[l] for l in range(depth)], v_out=[vs_out[l] for l in range(depth)]))
    for ti, tl in enumerate(tiles):
        direct[0] = (ti == 0)
        nxt = (tiles[ti + 1]["x"], tiles[ti + 1]["T"]) if ti + 1 < len(tiles) else None
        run_tile(tl["T"], tl["x"], tl["y"], tl["first"], tl["s0"], tl["s_out"], tl["v_out"], prefetched=(ti > 0), nxt=nxt)
    P.finish()
    P.run()
    print("sbuf bytes remaining", nc.sbuf_bytes_remaining)
    return nc, P


_CACHE = {}


def _prep_weights(inp, depth):
    f = lambda a: np.ascontiguousarray(np.asarray(a, dtype=np.float32))
    m = {
        "w1": f(inp["ffn_w1"][:depth]).reshape(depth * 2 * D, DFF),
        "w3": f(inp["ffn_w3"][:depth]).reshape(depth * 2 * D, DFF),
        "w2": f(inp["ffn_w2"][:depth]).reshape(depth * 2 * DFF, D),
        "win": f(inp["w_in"][:depth]).reshape(depth * D, INC),
        "wpa": f(inp["w_pa"][:depth]).reshape(depth * 512, D),
        "wpb": f(inp["w_pb"][:depth]).reshape(depth * D, D),
        "wo": f(inp["w_o"][:depth]).reshape(depth * D, D),
        "lng": f(inp["ln_g"][:depth]).reshape(depth * 24, 128),
        "lnb": f(inp["ln_b"][:depth]).reshape(depth * 24, 128),
        "gng": f(inp["gla_norm_g"][:depth]).reshape(depth * 2, 128),
        "gmg": f(inp["gm_ln_g"][:depth]), "gmb": f(inp["gm_ln_b"][:depth]),
        "bs": f(inp["gm_bs"][:depth]).reshape(depth, 512),
        "ba": f(inp["gla_ba"][:depth]),
        "ws": f(inp["gm_ws"][:depth]).reshape(depth * 4, 128, 128),
        "wa2": f(inp["gla_wa2"][:depth]),
    }
    return m


def run(inp, depth, nseq, seqlen, has_sample, ncores):
    key = (depth, nseq, seqlen, has_sample)
    if key not in _CACHE:
        _CACHE[key] = build(depth, nseq, seqlen, has_sample)
    nc, P = _CACHE[key]
    wm = _prep_weights(inp, depth)
    xp_all = np.asarray(inp["x_prompt"], dtype=np.float32)
    xs_all = np.asarray(inp["x_sample"], dtype=np.float32)
    st_all = np.asarray(inp["state_gla"], dtype=np.float32)
    in_maps = []
    for c in range(ncores):
        m = dict(wm)
        m["xp"] = np.ascontiguousarray(xp_all[c * nseq:(c + 1) * nseq, :seqlen]).reshape(nseq * seqlen, D)
        m["xs"] = np.ascontiguousarray(xs_all[c])
        m["st0"] = np.ascontiguousarray(st_all[:depth, c])
        in_maps.append(m)
    res = run_bass_kernel_spmd(nc, in_maps, core_ids=list(range(ncores)))
    R = res.results
    yp = np.concatenate([r["yp"].reshape(nseq, seqlen, D) for r in R], axis=0)
    ys = np.stack([r["ys"] for r in R], axis=0)
    spo = np.concatenate([r["sp_out"] for r in R], axis=1)
    sso = np.stack([r["ss_out"] for r in R], axis=1)
    vso = np.stack([r["vs_out"] for r in R], axis=1)
    return (yp.astype(np.float32), ys.astype(np.float32), spo.astype(np.float32), sso.astype(np.float32), vso.astype(np.float32))


def kernel(**inputs):
    return run(inputs, DEPTH, 2, 2048, True, 8)
```
